# Optimizing a Trainium2 kernel written in Bass

```python
import math
import jax, jax.numpy as jnp
from jax import lax
import numpy as np

D_MODEL = 1024
BATCH = 8
SEQ = 4096
DEPTH = 2

N_MIXERS = 2
EPS = 1e-6
CONV_WIDTH = 3
WINDOWS = (128, 512, 2048)
DILATIONS = (1, 4, 16)
N_GROUPS = 3
HEADS_PER_GROUP = 8
HEAD_DIM = 64
GROUP_WIDTH = HEADS_PER_GROUP * HEAD_DIM
N_ATTN_HEADS = N_GROUPS * HEADS_PER_GROUP
QKV_WIDTH = 3 * N_GROUPS * GROUP_WIDTH
Q_BLOCK = 128
NEG_INF = -1e30
N_BUCKETS = 32
MAX_DISTANCE = 1024
D_FF = 2816
N_A = (DEPTH + N_MIXERS - 1) // N_MIXERS
N_B = DEPTH // N_MIXERS

kernel_name = "hybrid_shortconv_dilated_attn_encoder"


def rmsnorm(x, w):
    xf = x.astype(jnp.float32)
    y = xf * lax.rsqrt(jnp.mean(xf * xf, axis=-1, keepdims=True) + EPS)
    return (y * w.astype(jnp.float32)).astype(x.dtype)


def dwconv3(x, w):
    return lax.conv_general_dilated(
        x, w[:, None, :].astype(x.dtype), window_strides=(1,),
        padding=[(CONV_WIDTH // 2, CONV_WIDTH // 2)],
        dimension_numbers=("NWC", "WIO", "NWC"),
        feature_group_count=x.shape[-1])


def t5_bucket(rel):
    half = N_BUCKETS // 2
    max_exact = half // 2
    n = jnp.abs(rel)
    side = jnp.where(rel > 0, half, 0)
    nf = jnp.maximum(n, 1).astype(jnp.float32)
    large = max_exact + (jnp.log(nf / max_exact) / math.log(MAX_DISTANCE / max_exact)
                         * (half - max_exact)).astype(jnp.int32)
    large = jnp.minimum(large, half - 1)
    return side + jnp.where(n < max_exact, n, large)


def short_conv_mixer(xn, w_in, conv_w, w_out):
    b_g, c_g, h = jnp.split(xn @ w_in, 3, axis=-1)
    return (b_g * dwconv3(c_g * h, conv_w)) @ w_out


def dilated_attention(xn, w_qkv, w_out, rel_bias):
    B_, S_, _ = xn.shape
    qkv = (xn @ w_qkv).reshape(B_, S_, 3, N_GROUPS, HEADS_PER_GROUP, HEAD_DIM)
    q = qkv[:, :, 0] * (HEAD_DIM ** -0.5)
    k = qkv[:, :, 1]
    v = qkv[:, :, 2]
    qs = [q[:, :, g] for g in range(N_GROUPS)]
    ks = [k[:, :, g] for g in range(N_GROUPS)]
    vs = [v[:, :, g] for g in range(N_GROUPS)]
    offs, biases = [], []
    for g in range(N_GROUPS):
        half = WINDOWS[g] // (2 * DILATIONS[g])
        o_g = DILATIONS[g] * jnp.arange(-half, half + 1, dtype=jnp.int32)
        offs.append(o_g)
        cols = rel_bias[t5_bucket(o_g)][:, g * HEADS_PER_GROUP:(g + 1) * HEADS_PER_GROUP]
        biases.append(cols.astype(jnp.float32).T[None, :, None, :])

    def block(q0):
        qpos = q0 + jnp.arange(Q_BLOCK, dtype=jnp.int32)
        outs, lses = [], []
        for g in range(N_GROUPS):
            qg = lax.dynamic_slice_in_dim(qs[g], q0, Q_BLOCK, axis=1)
            kpos = qpos[:, None] + offs[g][None, :]
            valid = (kpos >= 0) & (kpos < S_)
            kidx = jnp.clip(kpos, 0, S_ - 1)
            kg = jnp.take(ks[g], kidx, axis=1)
            vg = jnp.take(vs[g], kidx, axis=1)
            s = jnp.einsum("bqhd,bqkhd->bhqk", qg, kg,
                           preferred_element_type=jnp.float32) + biases[g]
            s = jnp.where(valid[None, None], s, NEG_INF)
            m = jnp.max(s, axis=-1, keepdims=True)
            e = jnp.exp(s - m)
            den = jnp.sum(e, axis=-1)
            o = jnp.einsum("bhqk,bqkhd->bqhd", e, vg.astype(jnp.float32))
            outs.append(o / jnp.transpose(den, (0, 2, 1))[..., None])
            lses.append(m[..., 0] + jnp.log(den))
        alpha = jax.nn.softmax(jnp.stack(lses), axis=0)
        o = jnp.einsum("gbhq,gbqhd->bqhd", alpha, jnp.stack(outs))
        return o.astype(xn.dtype)

    starts = jnp.arange(S_ // Q_BLOCK, dtype=jnp.int32) * Q_BLOCK
    o = lax.map(block, starts)
    o = jnp.moveaxis(o, 0, 1).reshape(B_, S_, GROUP_WIDTH)
    return o @ w_out


def conv_ffn(xn, w_up, conv_w, conv_b, w_down):
    g, u = jnp.split(xn @ w_up, 2, axis=-1)
    g = dwconv3(g, conv_w) + conv_b
    return (jax.nn.silu(g) * u) @ w_down


def setup_inputs(seed: int = 0) -> dict:
    key = jax.random.key(seed)
    ks = jax.random.split(key, 14)
    f32 = jnp.float32
    D, F = D_MODEL, D_FF
    nrm = lambda k, shape, s: jax.random.normal(k, shape, f32) * s
    return {
        "x": nrm(ks[0], (BATCH, SEQ, D), 1.0),
        "norm_w": 1.0 + nrm(ks[1], (DEPTH, 2, D), 0.02),
        "conv_in": nrm(ks[2], (N_A, D, 3 * D), D ** -0.5),
        "conv_w": nrm(ks[3], (N_A, CONV_WIDTH, D), CONV_WIDTH ** -0.5),
        "conv_out": nrm(ks[4], (N_A, D, D), D ** -0.5),
        "attn_qkv": nrm(ks[5], (N_B, D, QKV_WIDTH), D ** -0.5),
        "attn_out": nrm(ks[6], (N_B, GROUP_WIDTH, D), GROUP_WIDTH ** -0.5),
        "rel_bias": nrm(ks[7], (N_BUCKETS, N_ATTN_HEADS), 0.2),
        "ffn_up": nrm(ks[8], (DEPTH, D, 2 * F), D ** -0.5),
        "ffn_conv_w": nrm(ks[9], (DEPTH, CONV_WIDTH, F), CONV_WIDTH ** -0.5),
        "ffn_conv_b": nrm(ks[10], (DEPTH, F), 0.02),
        "ffn_down": nrm(ks[11], (DEPTH, F, D), F ** -0.5),
        "final_norm": 1.0 + nrm(ks[12], (D,), 0.02),
    }


def reference(x, norm_w, conv_in, conv_w, conv_out, attn_qkv, attn_out, rel_bias,
              ffn_up, ffn_conv_w, ffn_conv_b, ffn_down, final_norm):
    for i in range(DEPTH):
        j = i // N_MIXERS
        h = rmsnorm(x, norm_w[i, 0])
        if i % N_MIXERS == 0:
            x = x + short_conv_mixer(h, conv_in[j], conv_w[j], conv_out[j])
        else:
            x = x + dilated_attention(h, attn_qkv[j], attn_out[j], rel_bias)
        h = rmsnorm(x, norm_w[i, 1])
        x = x + conv_ffn(h, ffn_up[i], ffn_conv_w[i], ffn_conv_b[i], ffn_down[i])
    return rmsnorm(x, final_norm)
```

```python
import contextlib
import numpy as np
import concourse.bass as bass
import concourse.mybir as mybir
from concourse.bass_utils import run_bass_kernel_spmd

F32 = mybir.dt.float32
BF16 = mybir.dt.bfloat16
AF = mybir.ActivationFunctionType
ALU = mybir.AluOpType

D = 1024
SEQ = 4096
FF = 2816
KC = 8
FCH = 22
EPS = 1e-6
NEG = -30000.0
DIL = (1, 4, 16)
NV = 240


class Res:
    __slots__ = ("name", "w", "r")

    def __init__(self, name=""):
        self.name = name
        self.w = None
        self.r = []


class Op:
    __slots__ = ("eng", "fn", "deps", "dma", "idx", "sig", "needs_inc")

    def __init__(self, eng, fn, dma):
        self.eng = eng
        self.fn = fn
        self.dma = dma
        self.deps = []
        self.idx = -1
        self.sig = None
        self.needs_inc = False


class Sched:
    ENGS = ("pe", "act", "dve", "pool", "sp")
    HANDLE = {"pe": "tensor", "act": "scalar", "dve": "vector", "pool": "gpsimd", "sp": "sync"}
    SEM_CAP = 20000

    def __init__(self, nc, n_dma_sems=64):
        self.nc = nc
        self.ops = {e: [] for e in self.ENGS}
        self.n_dma_sems = n_dma_sems
        self.dma_last = [None] * n_dma_sems
        self.dma_rr = {False: 0, True: 0}
        self.dma_order = []
        self.pending_dma = []
        self.pending_barrier = {}

    def op(self, eng, fn, reads=(), writes=(), dma=False, extra_deps=()):
        o = Op(eng, fn, dma)
        deps = []
        seen = set()

        def add(d):
            if d is not None and id(d) not in seen:
                seen.add(id(d))
                deps.append(d)
        for r in reads:
            add(r.w)
        for r in writes:
            add(r.w)
            for x in r.r:
                add(x)
        for d in extra_deps:
            add(d)
        if eng in self.pending_barrier:
            add(self.pending_barrier.pop(eng))
        if dma:
            sw = (eng == "pool")
            half = self.n_dma_sems // 2
            k = self.dma_rr[sw] + (half if sw else 0)
            self.dma_rr[sw] = (self.dma_rr[sw] + 1) % half
            add(self.dma_last[k])
            self.dma_last[k] = o
            o.sig = k
            self.dma_order.append(o)
            self.pending_dma.append(o)
        for r in reads:
            r.r.append(o)
        for r in writes:
            r.w = o
            r.r = []
        o.idx = len(self.ops[eng])
        pr = []
        for d in deps:
            if (not d.dma) and d.eng == eng:
                if eng == "pe":
                    continue
            pr.append(d)
        o.deps = pr
        for d in pr:
            d.needs_inc = True
        self.ops[eng].append(o)
        return o

    def barrier(self, dummy):
        deps = [self.ops[e][-1] for e in self.ENGS if self.ops[e]] + list(self.pending_dma)
        self.pending_dma = []
        j = self.op("dve", lambda e: e.memset(dummy[:, 0:1], 0.0), extra_deps=deps)
        self.pending_barrier = {e: j for e in self.ENGS if e != "dve"}
        return j

    def emit(self):
        nc = self.nc
        with contextlib.ExitStack() as st:
            for e in self.ENGS:
                n = sum(1 for o in self.ops[e] if o.needs_inc and not o.dma)
                nsem = max(1, (n + self.SEM_CAP - 1) // self.SEM_CAP)
                sems = [st.enter_context(nc.semaphore(f"s_{e}{i}")) for i in range(nsem)]
                c = 0
                for o in self.ops[e]:
                    if o.needs_inc and not o.dma:
                        o.sig = (sems[c // self.SEM_CAP], c % self.SEM_CAP + 1)
                        c += 1
            dsems = [st.enter_context(nc.semaphore(f"s_dma{i}")) for i in range(self.n_dma_sems)]
            dcount = [0] * self.n_dma_sems
            for o in self.dma_order:
                k = o.sig
                dcount[k] += 16
                o.sig = (dsems[k], dcount[k])
            block = st.enter_context(nc.Block())
            sched = self

            def make(e):
                def body(engh):
                    waited = {}
                    for o in sched.ops[e]:
                        for d in o.deps:
                            sem, val = d.sig
                            key = id(sem)
                            if waited.get(key, 0) < val:
                                engh.wait_ge(sem, val)
                                waited[key] = val
                        inst = o.fn(engh)
                        if o.dma:
                            inst.then_inc(o.sig[0], 16)
                        elif o.needs_inc:
                            inst.then_inc(o.sig[0], 1)
                return body
            for e in self.ENGS:
                if self.ops[e]:
                    getattr(block, self.HANDLE[e])(make(e))


class Ring:
    def __init__(self, tiles):
        self.tiles = tiles
        self.res = [Res() for _ in tiles]
        self.i = 0

    def next(self):
        k = self.i % len(self.tiles)
        self.i += 1
        return self.tiles[k], self.res[k]


def split_sizes(total, n):
    base = total // n
    rem = total - base * n
    return [base + (1 if i < rem else 0) for i in range(n)]


def pieces_of(total, maxn=512):
    n = (total + maxn - 1) // maxn
    out = []
    a = 0
    for s in split_sizes(total, n):
        out.append((a, s))
        a += s
    return out


def build_program(stop_after="all"):
    nc = bass.Bass("TRN2", target_bir_lowering=False)
    S = Sched(nc)

    def din(name, shape, dt=F32):
        return nc.dram_tensor(name, shape, dt, kind="ExternalInput").ap()

    def dscr(name, shape, dt):
        return nc.dram_tensor(name, shape, dt, kind="Internal").ap()

    xT = din("xT", [D, SEQ])
    vecs_d = din("vecs", [128, NV])
    ident_d = din("ident", [128, 128])
    biasT_d = din("biasT", [128, 24, 256])
    w_ci = din("conv_in", [D, 3 * D])
    w_co = din("conv_out", [D, D])
    w_qkv = din("attn_qkv", [D, 4608])
    w_ao = din("attn_out", [512, D])
    w_up = din("ffn_up", [2, D, 2 * FF])
    w_dn = din("ffn_down", [2, FF, D])
    out = nc.dram_tensor("out", [D, SEQ], F32, kind="ExternalOutput").ap()

    wb_ci = dscr("wb_ci", [D, 3 * D], BF16)
    wb_co = dscr("wb_co", [D, D], BF16)
    wb_qkv = dscr("wb_qkv", [D, 4608], BF16)
    wb_ao = dscr("wb_ao", [512, D], BF16)
    wb_up = [dscr(f"wb_up{i}", [D, 2 * FF], BF16) for i in range(2)]
    wb_dn = [dscr(f"wb_dn{i}", [4, 128, FCH, 256], BF16) for i in range(2)]
    x1_d = dscr("x1_d", [D, SEQ], F32)
    x2_d = dscr("x2_d", [D, SEQ], F32)
    x3_d = dscr("x3_d", [D, SEQ], F32)
    ot_d = dscr("ot_d", [512, SEQ], BF16)
    hn_d = dscr("hn_d", [8, 128, KC, 512], BF16)

    r_wci = [Res(), Res(), Res()]
    r_wco, r_wao = Res(), Res()
    r_wqkv = [Res(), Res(), Res()]
    r_wup = [[Res(), Res()], [Res(), Res()]]
    r_wdn = [[Res() for _ in range(4)] for _ in range(2)]
    r_out = Res()

    def fm(ap2d, lo, hi):
        return ap2d[:, lo:hi].rearrange("(c p) t -> p c t", p=128)

    with contextlib.ExitStack() as gst:
        def sb(name, shape, dt, st=gst):
            return st.enter_context(nc.sbuf_tensor(name, shape, dt))

        banks = [gst.enter_context(nc.psum_tensor(f"bank{i}", [128, 512], F32)) for i in range(8)]
        r_bank = [Res(f"bank{i}") for i in range(8)]
        bank_rr = [0]

        def next_bank(pool=None):
            pool = pool or list(range(8))
            k = pool[bank_rr[0] % len(pool)]
            bank_rr[0] += 1
            return banks[k], r_bank[k]

        vecs = sb("vecs_sb", [128, NV], F32)
        ident_f = sb("ident_f", [128, 128], F32)
        ident = sb("ident_sb", [128, 128], BF16)
        ones = sb("ones_sb", [128, 128], BF16)
        dummy = sb("dummy_sb", [128, 4], F32)
        r_vecs, r_identf, r_ident, r_ones = Res(), Res(), Res(), Res()
        S.op("sp", lambda e: e.dma_start(out=vecs[:], in_=vecs_d[:, :]), writes=[r_vecs], dma=True)
        S.op("sp", lambda e: e.dma_start(out=ident_f[:], in_=ident_d[:, :]), writes=[r_identf], dma=True)
        S.op("dve", lambda e: e.tensor_copy(out=ident[:], in_=ident_f[:]), reads=[r_identf], writes=[r_ident])
        S.op("dve", lambda e: e.memset(ones[:], 1.0), writes=[r_ones])

        def vcol(j):
            return vecs[:, j:j + 1]

        cast_prev = [None]
        cast_gate = [None]

        def cast(dst, src, res, chain=True):
            deps = [d for d in ((cast_prev[0] if chain else None), cast_gate[0]) if d is not None]
            cast_prev[0] = S.op("pool", lambda e: e.dma_start(out=dst, in_=src, max_dma_last_dim=4096), writes=[res], dma=True,
                                extra_deps=deps)
        for j3 in range(3):
            cast(wb_ci[:, j3 * D:(j3 + 1) * D], w_ci[:, j3 * D:(j3 + 1) * D], r_wci[j3])
        cast(wb_co[:, :], w_co[:, :], r_wco)

        def cast_ffn(i):
            up = wb_up[i].rearrange("r (k four m) -> r k four m", four=4, m=128)
            cast(up[:, :, 0:2, :], w_up[i, :, 0:FF].rearrange("r (k two m) -> r k two m", two=2, m=128), r_wup[i][0])
            cast(up[:, :, 2:4, :], w_up[i, :, FF:2 * FF].rearrange("r (k two m) -> r k two m", two=2, m=128), r_wup[i][1])
            for og in range(4):
                cast(wb_dn[i][og, :, :, :], w_dn[i, :, og * 256:(og + 1) * 256].rearrange("(fc p) m -> p fc m", p=128), r_wdn[i][og])
        cast_ffn(0)

        def late_casts():
            qv = wb_qkv.rearrange("r (gp j m) -> r gp j m", j=3, m=128)
            for j in range(3):
                cast(qv[:, :, j, :], w_qkv[:, j * 1536:(j + 1) * 1536].rearrange("r (gp m) -> r gp m", m=128), r_wqkv[j])
            cast(wb_ao[:, :], w_ao[:, :], r_wao)
            cast_ffn(1)


        def load_window(dst, rdst, src2d, t0, ncols, eng="sp", mem_eng="pool"):
            lo = max(t0, 0)
            hi = min(t0 + ncols, SEQ)
            if lo > t0:
                S.op(mem_eng, lambda e: e.memset(dst[:, :, 0:lo - t0], 0.0), writes=[rdst])
            if hi < t0 + ncols:
                S.op(mem_eng, lambda e: e.memset(dst[:, :, hi - t0:ncols], 0.0), writes=[rdst])
            S.op(eng, lambda e: e.dma_start(out=dst[:, :, lo - t0:hi - t0], in_=fm(src2d, lo, hi)),
                 writes=[rdst], dma=True)

        def rms_sq(xw, rxw, sq, rsq, ncols):
            S.op("act", lambda e: e.activation(out=sq[:, :, 0:ncols], in_=xw[:, :, 0:ncols], func=AF.Square),
                 reads=[rxw], writes=[rsq])

        def rms_fin(sq, rsq, rstd, rrstd, ncols, roff=0):
            bk, rbk = next_bank()

            def mm(e):
                for c in range(KC):
                    i = e.matmul(bk[:, 0:ncols], lhsT=ones[:, :], rhs=sq[:, c, 0:ncols], start=(c == 0), stop=(c == KC - 1))
                return i
            S.op("pe", mm, reads=[rsq, r_ones], writes=[rbk])
            S.op("act", lambda e: e.activation(out=rstd[:, roff:roff + ncols], in_=bk[:, 0:ncols], func=AF.Ln,
                                               scale=1.0 / D, bias=EPS), writes=[rbk, rrstd])
            S.op("act", lambda e: e.activation(out=rstd[:, roff:roff + ncols], in_=rstd[:, roff:roff + ncols],
                                               func=AF.Exp, scale=-0.5), writes=[rrstd])

        def rms_stats(xw, rxw, sq, rsq, rstd, rrstd, ncols, roff=0):
            rms_sq(xw, rxw, sq, rsq, ncols)
            rms_fin(sq, rsq, rstd, rrstd, ncols, roff)

        def rms_apply(xw, rxw, rstd, rrstd, xn, rxn, ncols, wcol, roff=0, xoff=0):
            def f(e):
                for c in range(KC):
                    i = e.scalar_tensor_tensor(out=xn[:, c, xoff:xoff + ncols], in0=xw[:, c, 0:ncols], scalar=vcol(wcol + c),
                                               in1=rstd[:, roff:roff + ncols], op0=ALU.mult, op1=ALU.mult)
                return i
            S.op("dve", f, reads=[rxw, rrstd, r_vecs], writes=[rxn])

        def make_diag(diag, rdiag, ncol, colbase):
            def f(e):
                for j in range(ncol):
                    i = e.tensor_scalar(out=diag[:, j, :], in0=ident_f[:, :], scalar1=vcol(colbase + j), scalar2=None, op0=ALU.mult)
                return i
            S.op("dve", f, reads=[r_identf, r_vecs], writes=[rdiag])

        def phase_A(dst2d):
            with contextlib.ExitStack() as st:
                WM = 458
                xr = Ring([sb(f"A_x{i}", [128, KC, WM], F32, st) for i in range(2)])
                sq = sb("A_sq", [128, KC, WM], BF16, st)
                rsq = Res()
                rstd = sb("A_rstd", [128, WM], F32, st)
                rrstd = Res()
                xn = sb("A_xn", [128, KC, WM], BF16, st)
                rxn = Res()
                b_sb = sb("A_b", [128, KC, WM], BF16, st)
                rb = [Res() for _ in range(KC)]
                cr = Ring([sb(f"A_c{i}", [128, WM], F32, st) for i in range(KC)])
                z = sb("A_z", [128, KC, WM], BF16, st)
                rz = [Res() for _ in range(KC)]
                y = sb("A_y", [128, KC, WM], BF16, st)
                ry = [Res() for _ in range(KC)]
                wci_sb = sb("A_wci", [128, KC, 3 * D], BF16, st)
                rwci_sb = [[Res() for _ in range(KC)] for _ in range(3)]

                def wci_load(j3):
                    for kc in range(KC):
                        S.op("sp", lambda e, j3=j3, kc=kc: e.dma_start(
                            out=wci_sb[:, kc, j3 * D:(j3 + 1) * D],
                            in_=wb_ci[kc * 128:(kc + 1) * 128, j3 * D:(j3 + 1) * D]),
                            reads=[r_wci[j3]], writes=[rwci_sb[j3][kc]], dma=True)
                wco = sb("A_wco", [128, KC, D], BF16, st)
                rwco = Res()
                diag = sb("A_diag", [128, 24, 128], BF16, st)
                rdiag = Res()
                make_diag(diag, rdiag, 24, 40)
                sizes = split_sizes(SEQ, 9)
                tstarts = [sum(sizes[:i]) for i in range(9)]
                xn2 = [xn, sb("A_xn2", [128, KC, WM], BF16, st)]
                rxn2 = [Res(), Res()]
                xws = {}

                def pro_a(i):
                    xw, rxw = xr.next()
                    xws[i] = (xw, rxw)
                    load_window(xw, rxw, xT, tstarts[i] - 1, sizes[i] + 2, mem_eng=("dve" if i == 0 else "pool"))
                    rms_sq(xw, rxw, sq, rsq, sizes[i] + 2)

                def pro_b(i):
                    xw, rxw = xws[i]
                    rms_fin(sq, rsq, rstd, rrstd, sizes[i] + 2)
                    rms_apply(xw, rxw, rstd, rrstd, xn2[i % 2], rxn2[i % 2], sizes[i] + 2, 0)
                pro_a(0)
                wci_load(0)
                pro_b(0)
                wci_load(1)
                wci_load(2)
                pending = []
                for ti, T in enumerate(sizes):
                    t0 = tstarts[ti]
                    W = T + 2
                    xw, rxw = xws[ti]
                    xn, rxn = xn2[ti % 2], rxn2[ti % 2]
                    while pending:
                        pending.pop()()
                    def inproj(col0, rw):
                        bk, rbk = next_bank()

                        def mm(e, bk=bk, col0=col0, W=W, xn=xn):
                            for c in range(KC):
                                i = e.matmul(bk[:, 0:W], lhsT=wci_sb[:, c, col0:col0 + 128], rhs=xn[:, c, 0:W],
                                             start=(c == 0), stop=(c == KC - 1))
                            return i
                        S.op("pe", mm, reads=list(rw) + [rxn], writes=[rbk])
                        return bk, rbk
                    for k in range(KC):
                        bk, rbk = inproj(k * 128, rwci_sb[0])
                        S.op("act", lambda e, bk=bk, k=k, W=W: e.activation(out=b_sb[:, k, 0:W], in_=bk[:, 0:W], func=AF.Copy),
                             writes=[rbk, rb[k]])
                    cts = []
                    for k in range(KC):
                        bk, rbk = inproj(D + k * 128, rwci_sb[1])
                        ct, rct = cr.next()
                        cts.append((ct, rct))
                        S.op("act", lambda e, bk=bk, ct=ct, W=W: e.activation(out=ct[:, 0:W], in_=bk[:, 0:W], func=AF.Copy),
                             writes=[rbk, rct])
                    for k in range(KC):
                        ct, rct = cts[k]
                        bk, rbk = inproj(2 * D + k * 128, rwci_sb[2])
                        S.op("dve", lambda e, bk=bk, ct=ct, k=k, W=W: e.tensor_tensor(
                            out=z[:, k, 0:W], in0=bk[:, 0:W], in1=ct[:, 0:W], op=ALU.mult),
                            reads=[rct], writes=[rbk, rz[k]])
                    if ti == 0:
                        S.op("sp", lambda e: e.dma_start(out=wco[:], in_=wb_co.rearrange("(c p) n -> p c n", p=128)),
                             reads=[r_wco], writes=[rwco], dma=True)
                    while pending:
                        pending.pop()()
                    if ti + 1 < 9:
                        pro_a(ti + 1)
                    for k in range(KC):
                        bk, rbk = next_bank()

                        def cv(e, bk=bk, k=k, T=T):
                            for tap in range(3):
                                i = e.matmul(bk[:, 0:T], lhsT=diag[:, tap * 8 + k, :], rhs=z[:, k, tap:tap + T],
                                             start=(tap == 0), stop=(tap == 2))
                            return i
                        S.op("pe", cv, reads=[rdiag, rz[k]], writes=[rbk])
                        S.op("dve", lambda e, bk=bk, k=k, T=T: e.tensor_tensor(
                            out=y[:, k, 0:T], in0=bk[:, 0:T], in1=b_sb[:, k, 1:T + 1], op=ALU.mult),
                            reads=[rb[k]], writes=[rbk, ry[k]])
                    if ti + 1 < 9:
                        pro_b(ti + 1)
                    for oc in range(KC):
                        bk, rbk = next_bank()

                        def mm(e, bk=bk, oc=oc, T=T):
                            for c in range(KC):
                                i = e.matmul(bk[:, 0:T], lhsT=wco[:, c, oc * 128:(oc + 1) * 128], rhs=y[:, c, 0:T],
                                             start=(c == 0), stop=(c == KC - 1))
                            return i
                        S.op("pe", mm, reads=[rwco] + ry, writes=[rbk])
                        S.op("dve", lambda e, bk=bk, oc=oc, xw=xw, T=T: e.tensor_tensor(
                            out=xw[:, oc, 1:T + 1], in0=bk[:, 0:T], in1=xw[:, oc, 1:T + 1], op=ALU.add),
                            writes=[rbk, rxw])
                    pending.append(lambda xw=xw, rxw=rxw, t0=t0, T=T: S.op(
                        "act", lambda e: e.dma_start(out=fm(dst2d, t0, t0 + T), in_=xw[:, :, 1:T + 1]),
                        reads=[rxw], writes=[r_out], dma=True))
                while pending:
                    pending.pop()()
                return S.barrier(dummy)

        def phase_ffn(li, src2d, dst2d, attn=False, final=False, mid2d=None, hn2d=None):
            with contextlib.ExitStack() as st:
                nmt = 3
                msizes = split_sizes(SEQ, nmt)
                starts = [sum(msizes[:i]) for i in range(nmt)]
                TM = max(msizes)
                WM = TM + 2
                PM = 458
                xr = Ring([sb(f"F{li}_x{i}", [128, KC, PM], F32, st) for i in range(2)])
                xp0 = sb(f"F{li}_xp", [128, KC, PM], F32, st)
                rxp0 = Res()
                sq = sb(f"F{li}_sq", [128, KC, PM], BF16, st)
                rsq = Res()
                rstd = sb(f"F{li}_rstd", [128, WM], F32, st)
                rrstd = [Res() for _ in range(4)]
                rstd_e = sb(f"F{li}_rstde", [128, PM], F32, st)
                xn = sb(f"F{li}_xn", [128, KC, WM], BF16, st)
                rxn = [Res() for _ in range(4)]
                wr = Ring([sb(f"F{li}_w{i}", [128, KC, 512], BF16, st) for i in range(2)])
                wdr = Ring([sb(f"F{li}_wd{i}", [128, FCH, 256], BF16, st) for i in range(2)])
                a = sb(f"F{li}_a", [128, FCH, TM], BF16, st)
                ra = [Res() for _ in range(FCH)]
                gr = Ring([sb(f"F{li}_g{i}", [128, WM], BF16, st) for i in range(2)])
                sr = Ring([sb(f"F{li}_s{i}", [128, PM], BF16, st) for i in range(3)])
                t2r = Ring([sb(f"F{li}_t{i}", [128, PM], F32, st) for i in range(2)])
                dr = Ring([sb(f"F{li}_dg{i}", [128, 3, 128], BF16, st) for i in range(3)])
                r_mid = []
                if attn:
                    wao = sb(f"F{li}_wao", [128, 4, D], BF16, st)
                    rwao = Res()
                    S.op("sp", lambda e: e.dma_start(out=wao[:], in_=wb_ao.rearrange("(c p) n -> p c n", p=128)),
                         reads=[r_wao], writes=[rwao], dma=True)
                    otr = Ring([sb(f"F{li}_ot{i}", [128, 4, PM], BF16, st) for i in range(2)])
                nwcol = (li * 2 + 1) * 8
                fcw = 64 + li * 3 * FCH
                fcb = 196 + li * FCH

                def xn_res(wp, lo, hi):
                    return [rxn[i] for i, (a0, n) in enumerate(wp) if a0 < hi and a0 + n > lo]

                xbuf = {}

                def pro_load(mi, pi, buf=None):
                    t0, T = starts[mi], msizes[mi]
                    (a0, n) = pieces_of(T + 2)[pi]
                    xp, rxp = buf if buf is not None else (xp0, rxp0)
                    xbuf[(mi, pi)] = (xp, rxp)
                    load_window(xp, rxp, src2d, t0 - 1 + a0, n, mem_eng=("dve" if mi == 0 else "pool"))

                def pro_a(mi, pi, buf=None):
                    t0, T = starts[mi], msizes[mi]
                    (a0, n) = pieces_of(T + 2)[pi]
                    if (mi, pi) not in xbuf:
                        pro_load(mi, pi, buf)
                    xp, rxp = xbuf[(mi, pi)]
                    if attn:
                        ot, rot = otr.next()
                        lo = max(t0 - 1 + a0, 0)
                        hi = min(t0 - 1 + a0 + n, SEQ)
                        o0 = lo - (t0 - 1 + a0)
                        if o0 > 0:
                            S.op("dve" if mi == 0 else "pool", lambda e, ot=ot, o0=o0: e.memset(ot[:, :, 0:o0], 0.0), writes=[rot])
                        if hi - (t0 - 1 + a0) < n:
                            S.op("dve" if mi == 0 else "pool", lambda e, ot=ot, h=hi - (t0 - 1 + a0), n=n: e.memset(ot[:, :, h:n], 0.0), writes=[rot])
                        S.op("sp", lambda e, ot=ot, lo=lo, hi=hi, o0=o0: e.dma_start(
                            out=ot[:, :, o0:o0 + hi - lo], in_=ot_d[:, lo:hi].rearrange("(c p) t -> p c t", p=128)),
                            writes=[rot], dma=True)
                        for oc in range(KC):
                            bk, rbk = next_bank()

                            def mm(e, bk=bk, oc=oc, ot=ot, n=n):
                                for c in range(4):
                                    i = e.matmul(bk[:, 0:n], lhsT=wao[:, c, oc * 128:(oc + 1) * 128], rhs=ot[:, c, 0:n],
                                                 start=(c == 0), stop=(c == 3))
                                return i
                            S.op("pe", mm, reads=[rwao, rot], writes=[rbk])
                            S.op("dve", lambda e, bk=bk, oc=oc, n=n, xp=xp: e.tensor_tensor(
                                out=xp[:, oc, 0:n], in0=bk[:, 0:n], in1=xp[:, oc, 0:n], op=ALU.add),
                                writes=[rbk, rxp])
                        m_lo = max(a0, 1)
                        m_hi = min(a0 + n, T + 1)
                        rm = Res()
                        r_mid.append(rm)
                        S.op("act", lambda e, m_lo=m_lo, m_hi=m_hi, a0=a0, t0=t0, xp=xp: e.dma_start(
                            out=fm(mid2d, t0 - 1 + m_lo, t0 - 1 + m_hi), in_=xp[:, :, m_lo - a0:m_hi - a0]),
                            reads=[rxp], writes=[rm], dma=True)
                    rms_sq(xp, rxp, sq, rsq, n)

                def pro_b(mi, pi):
                    T = msizes[mi]
                    (a0, n) = pieces_of(T + 2)[pi]
                    xp, rxp = xbuf[(mi, pi)]
                    rms_fin(sq, rsq, rstd, rrstd[pi], n, roff=a0)
                    rms_apply(xp, rxp, rstd, rrstd[pi], xn, rxn[pi], n, nwcol, roff=a0, xoff=a0)

                def up_proj(mi):
                    T = msizes[mi]
                    W = T + 2
                    wp = pieces_of(W)
                    mp = pieces_of(T)
                    for k in range(FCH // 2):
                        wt, rwt = wr.next()
                        S.op("sp", lambda e, wt=wt, k=k: e.dma_start(
                            out=wt[:], in_=wb_up[li][:, k * 512:(k + 1) * 512].rearrange("(c p) n -> p c n", p=128)),
                            reads=r_wup[li], writes=[rwt], dma=True)
                        if k == FCH // 2 - 1:
                            prefetch_down(mi)
                        for q in range(2):
                            f = 2 * k + q
                            gt, rgt = gr.next()
                            dg, rdg = dr.next()

                            def mk(e, dg=dg, f=f):
                                for tap in (0, 2):
                                    i = e.tensor_scalar(out=dg[:, tap, :], in0=ident_f[:, :], scalar1=vcol(fcw + tap * FCH + f),
                                                        scalar2=None, op0=ALU.mult)
                                return i
                            S.op("dve", mk, reads=[r_identf, r_vecs], writes=[rdg])
                            for pi, (a0, n) in enumerate(wp):
                                bk, rbk = next_bank()

                                def mm(e, bk=bk, wt=wt, q=q, a0=a0, n=n):
                                    for c in range(KC):
                                        i = e.matmul(bk[:, 0:n], lhsT=wt[:, c, q * 128:(q + 1) * 128], rhs=xn[:, c, a0:a0 + n],
                                                     start=(c == 0), stop=(c == KC - 1))
                                    return i
                                S.op("pe", mm, reads=[rwt, rxn[pi]], writes=[rbk])
                                S.op("dve", lambda e, bk=bk, gt=gt, a0=a0, n=n: e.tensor_copy(out=gt[:, a0:a0 + n], in_=bk[:, 0:n]),
                                     writes=[rbk, rgt])
                            ubanks = []
                            for (m0, n) in mp:
                                bk, rbk = next_bank()

                                def mm(e, bk=bk, wt=wt, q=q, m0=m0, n=n):
                                    for c in range(KC):
                                        i = e.matmul(bk[:, 0:n], lhsT=wt[:, c, (2 + q) * 128:(3 + q) * 128], rhs=xn[:, c, 1 + m0:1 + m0 + n],
                                                     start=(c == 0), stop=(c == KC - 1))
                                    return i
                                S.op("pe", mm, reads=[rwt] + xn_res(wp, 1 + m0, 1 + m0 + n), writes=[rbk])
                                ubanks.append((bk, rbk))
                            for pi, (m0, n) in enumerate(mp):
                                bk, rbk = next_bank()

                                def cv(e, bk=bk, dg=dg, gt=gt, m0=m0, n=n):
                                    for tap in (0, 2):
                                        i = e.matmul(bk[:, 0:n], lhsT=dg[:, tap, :], rhs=gt[:, m0 + tap:m0 + tap + n],
                                                     start=(tap == 0), stop=(tap == 2))
                                    return i
                                S.op("pe", cv, reads=[rdg, rgt], writes=[rbk])
                                t2, rt2 = t2r.next()
                                S.op("dve", lambda e, bk=bk, t2=t2, gt=gt, m0=m0, n=n, f=f: e.scalar_tensor_tensor(
                                    out=t2[:, 0:n], in0=gt[:, m0 + 1:m0 + 1 + n], scalar=vcol(fcw + FCH + f),
                                    in1=bk[:, 0:n], op0=ALU.mult, op1=ALU.add),
                                    reads=[rgt, r_vecs], writes=[rbk, rt2])
                                stt, rstt = sr.next()
                                S.op("act", lambda e, t2=t2, stt=stt, n=n, f=f: e.activation(
                                    out=stt[:, 0:n], in_=t2[:, 0:n], func=AF.Silu, bias=vcol(fcb + f)),
                                    reads=[r_vecs, rt2], writes=[rstt])
                                ub, rub = ubanks[pi]
                                S.op("dve", lambda e, ub=ub, stt=stt, f=f, m0=m0, n=n: e.tensor_tensor(
                                    out=a[:, f, m0:m0 + n], in0=ub[:, 0:n], in1=stt[:, 0:n], op=ALU.mult),
                                    reads=[rstt], writes=[rub, ra[f]])
                        if k == 0:
                            while deferred:
                                deferred.pop()()

                deferred = []

                pre_dn = {}

                def wd_load(og):
                    wd, rwd = wdr.next()
                    S.op("sp", lambda e, wd=wd, og=og: e.dma_start(out=wd[:], in_=wb_dn[li][og, :, :, :]),
                         reads=[r_wdn[li][og]], writes=[rwd], dma=True)
                    return wd, rwd

                def x_reload(xw, rxw, t0, m0, n):
                    S.op("sp", lambda e: e.dma_start(
                        out=xw[:, :, 0:n], in_=fm(mid2d if attn else src2d, t0 + m0, t0 + m0 + n)),
                        reads=list(r_mid), writes=[rxw], dma=True)

                def prefetch_down(mi):
                    (m0, n) = pieces_of(msizes[mi])[0]
                    xw, rxw = xr.next()
                    wds = {0: wd_load(0), 1: wd_load(1)}
                    x_reload(xw, rxw, starts[mi], m0, n)
                    pre_dn[mi] = (xw, rxw, wds)

                def down_proj(mi, nxt):
                    t0, T = starts[mi], msizes[mi]
                    mp = pieces_of(T)
                    for j, (m0, n) in enumerate(mp):
                        if j == 0 and mi in pre_dn:
                            xw, rxw, wds = pre_dn.pop(mi)
                        else:
                            xw, rxw = xr.next()
                            wds = {0: wd_load(0), 1: wd_load(1)}
                            x_reload(xw, rxw, t0, m0, n)
                        if nxt is not None:
                            pro_a(nxt, j)
                        for og in range(4):
                            if og >= 2:
                                wds[og] = wd_load(og)
                            wd, rwd = wds[og]
                            if j == 0 and og == 0:
                                CS = FCH - 2
                                grp = [next_bank(), next_bank()]
                                for q, (bk, rbk) in enumerate(grp):
                                    def mm1(e, bk=bk, wd=wd, q=q, m0=m0, n=n):
                                        for c in range(CS):
                                            i = e.matmul(bk[:, 0:n], lhsT=wd[:, c, q * 128:(q + 1) * 128], rhs=a[:, c, m0:m0 + n],
                                                         start=(c == 0), stop=False)
                                        return i
                                    S.op("pe", mm1, reads=[rwd] + ra[:CS], writes=[rbk])
                                for q, (bk, rbk) in enumerate(grp):
                                    oc = q

                                    def mm2(e, bk=bk, wd=wd, q=q, m0=m0, n=n):
                                        for c in range(CS, FCH):
                                            i = e.matmul(bk[:, 0:n], lhsT=wd[:, c, q * 128:(q + 1) * 128], rhs=a[:, c, m0:m0 + n],
                                                         start=False, stop=(c == FCH - 1))
                                        return i
                                    S.op("pe", mm2, reads=[rwd] + ra[CS:], writes=[rbk])
                                    S.op("dve", lambda e, bk=bk, oc=oc, xw=xw, n=n: e.tensor_tensor(
                                        out=xw[:, oc, 0:n], in0=bk[:, 0:n], in1=xw[:, oc, 0:n], op=ALU.add),
                                        writes=[rbk, rxw])
                                continue
                            for q in range(2):
                                oc = og * 2 + q
                                bk, rbk = next_bank()

                                def mm(e, bk=bk, wd=wd, q=q, m0=m0, n=n):
                                    for c in range(FCH):
                                        i = e.matmul(bk[:, 0:n], lhsT=wd[:, c, q * 128:(q + 1) * 128], rhs=a[:, c, m0:m0 + n],
                                                     start=(c == 0), stop=(c == FCH - 1))
                                    return i
                                S.op("pe", mm, reads=[rwd] + ra, writes=[rbk])
                                S.op("dve", lambda e, bk=bk, oc=oc, xw=xw, n=n: e.tensor_tensor(
                                    out=xw[:, oc, 0:n], in0=bk[:, 0:n], in1=xw[:, oc, 0:n], op=ALU.add),
                                    writes=[rbk, rxw])
                            if og == 1 and nxt is not None:
                                pro_b(nxt, j)
                            if og == 2 and deferred:
                                deferred.pop()()

                        def epilogue(xw=xw, rxw=rxw, m0=m0, n=n, t0=t0):
                            if not final:
                                S.op("act", lambda e: e.dma_start(out=fm(dst2d, t0 + m0, t0 + m0 + n), in_=xw[:, :, 0:n]),
                                     reads=[rxw], writes=[r_out], dma=True)
                            if final or hn2d is not None:
                                rms_stats(xw, rxw, sq, rsq, rstd_e, rrstd[3], n, roff=0)
                            if hn2d is not None:
                                def fh(e):
                                    for c in range(KC):
                                        i = e.scalar_tensor_tensor(out=sq[:, c, 0:n], in0=xw[:, c, 0:n], scalar=vcol(16 + c),
                                                                   in1=rstd_e[:, 0:n], op0=ALU.mult, op1=ALU.mult)
                                    return i
                                S.op("dve", fh, reads=[rxw, rrstd[3], r_vecs], writes=[rsq])
                                T0_, T1_ = t0 + m0, t0 + m0 + n
                                for n8 in range(T0_ // 512, (T1_ - 1) // 512 + 1):
                                    lo_, hi_ = max(T0_, 512 * n8), min(T1_, 512 * (n8 + 1))
                                    S.op("act", lambda e, n8=n8, lo_=lo_, hi_=hi_, T0_=T0_: e.dma_start(
                                        out=hn2d[n8, :, :, lo_ - 512 * n8:hi_ - 512 * n8], in_=sq[:, :, lo_ - T0_:hi_ - T0_]),
                                        reads=[rsq], writes=[Res()], dma=True)
                            if final:
                                def fo(e):
                                    for c in range(KC):
                                        i = e.scalar_tensor_tensor(out=xw[:, c, 0:n], in0=xw[:, c, 0:n], scalar=vcol(32 + c),
                                                                   in1=rstd_e[:, 0:n], op0=ALU.mult, op1=ALU.mult)
                                    return i
                                S.op("dve", fo, reads=[rrstd[3], r_vecs], writes=[rxw])
                            if final:
                                S.op("act", lambda e: e.dma_start(out=fm(dst2d, t0 + m0, t0 + m0 + n), in_=xw[:, :, 0:n]),
                                     reads=[rxw], writes=[r_out], dma=True)
                        deferred.append(epilogue)
                    if nxt is None:
                        while deferred:
                            deferred.pop()()

                init_bufs = [(xp0, rxp0), (xr.tiles[0], xr.res[0]), (xr.tiles[1], xr.res[1])]
                for pi in range(3):
                    pro_load(0, pi, init_bufs[pi])
                for pi in range(3):
                    pro_a(0, pi)
                    pro_b(0, pi)
                for mi in range(nmt):
                    up_proj(mi)
                    down_proj(mi, mi + 1 if mi + 1 < nmt else None)
                S.barrier(dummy)

        def phase_C():
            with contextlib.ExitStack() as st:
                hn = sb("C_hn", [128, KC, SEQ], BF16, st)
                rhn = Res()
                rstd = sb("C_rstd", [128, 512], F32, st)
                rrstd = Res()
                rhns = [Res() for _ in range(8)]
                def hn_load(n8, q):
                    S.op(q, lambda e, n8=n8: e.dma_start(out=hn[:, :, 512 * n8:512 * (n8 + 1)], in_=hn_d[n8, :, :, :]),
                         writes=[rhns[n8]], dma=True)
                hn_load(0, "act")
                biasT = sb("C_bias", [128, 24, 256], BF16, st)
                rbias = Res()
                S.op("pool", lambda e: e.dma_start(out=biasT[:], in_=biasT_d[:, :, :]), writes=[rbias], dma=True)
                e_h = sb("C_eh", [128, 2, 128], BF16, st)
                reh = Res()

                S.op("dve", lambda e: e.memset(e_h[:], 0.0), writes=[reh])
                S.op("dve", lambda e: e.memset(e_h[:, 0, 0:64], 1.0), writes=[reh])
                S.op("dve", lambda e: e.memset(e_h[:, 1, 64:128], 1.0), writes=[reh])
                qt = sb("C_qt", [128, SEQ], BF16, st)
                rqt = Res()
                kt = sb("C_kt", [128, 2, SEQ], BF16, st)
                rkt = Res()
                vp = sb("C_vp", [128, 32, 2, 128], BF16, st)
                rvp = Res()
                S.op("pool", lambda e: e.memset(kt[:], 0.0), writes=[rkt])
                S.op("pool", lambda e: e.memset(vp[:], 0.0), writes=[rvp])
                acc_n = sb("C_accn", [128, SEQ], F32, st)
                acc_d = sb("C_accd", [128, SEQ], F32, st)
                racc = Res()
                ob = sb("C_ob", [128, SEQ], BF16, st)
                rob = Res()
                wr = Ring([sb(f"C_w{i}", [128, KC, 384], BF16, st) for i in range(2)])
                pr = Ring([sb(f"C_p{i}", [128, 256], BF16, st) for i in range(8)])
                rot = [4, 5, 6, 7]
                for p in range(4):
                    for g in range(3):
                        d = DIL[g]
                        L = SEQ // d
                        nb = L // 128
                        gp = g * 4 + p
                        wt, rwt = wr.next()
                        S.op("sp", lambda e, wt=wt, gp=gp: e.dma_start(
                            out=wt[:], in_=wb_qkv[:, gp * 384:(gp + 1) * 384].rearrange("(c p) n -> p c n", p=128)),
                            reads=r_wqkv, writes=[rwt], dma=True)
                        if p == 0 and g == 0:
                            for n8 in range(1, 8):
                                hn_load(n8, "act" if n8 % 2 == 0 else "sp")
                        qv3 = qt[:, :].rearrange("p (r l) -> p r l", r=d)
                        for j in range(2):
                            for n in range(8):
                                bk, rbk = next_bank(rot)

                                def mm(e, bk=bk, wt=wt, j=j, n=n):
                                    for c in range(KC):
                                        i = e.matmul(bk[:, :], lhsT=wt[:, c, j * 128:(j + 1) * 128], rhs=hn[:, c, 512 * n:512 * (n + 1)],
                                                     start=(c == 0), stop=(c == KC - 1))
                                    return i
                                S.op("pe", mm, reads=[rwt, rhns[n]], writes=[rbk])
                                lw = 512 // d
                                src = bk[:, :].rearrange("p (l r) -> p r l", r=d)
                                if j == 0:
                                    S.op("act", lambda e, src=src, qv3=qv3, n=n, lw=lw: e.activation(
                                        out=qv3[:, :, lw * n:lw * (n + 1)], in_=src, func=AF.Copy, scale=0.125),
                                        writes=[rbk, rqt])
                                else:
                                    def ev(e, bk=bk, n=n, lw=lw, d=d):
                                        for hh in range(2):
                                            o = kt[hh * 64:(hh + 1) * 64, hh, :].rearrange("p (r l) -> p r l", r=d)
                                            s_ = bk[hh * 64:(hh + 1) * 64, :].rearrange("p (l r) -> p r l", r=d)
                                            i = e.tensor_copy(out=o[:, :, lw * n:lw * (n + 1)], in_=s_)
                                        return i
                                    S.op("dve", ev, writes=[rbk, rkt])
                        for b4 in range(8):
                            bk, rbk = next_bank(rot)

                            def mv(e, bk=bk, wt=wt, b4=b4, d=d, L=L):
                                for qq in range(4):
                                    kbi = b4 * 4 + qq
                                    r = (128 * kbi) // L
                                    l0 = (128 * kbi) % L
                                    s0 = r + d * l0
                                    for c in range(KC):
                                        i = e.matmul(bk[:, qq * 128:(qq + 1) * 128], lhsT=hn[:, c, s0:s0 + 127 * d + 1:d],
                                                     rhs=wt[:, c, 256:384], start=(c == 0), stop=(c == KC - 1))
                                return i
                            S.op("pe", mv, reads=[rwt] + rhns, writes=[rbk])

                            def evv(e, bk=bk, b4=b4):
                                s_ = bk[:, :].rearrange("p (q c) -> p q c", c=128)
                                e.tensor_copy(out=vp[:, 4 * b4:4 * b4 + 4, 0, 0:64], in_=s_[:, :, 0:64])
                                return e.tensor_copy(out=vp[:, 4 * b4:4 * b4 + 4, 1, 64:128], in_=s_[:, :, 64:128])
                            S.op("dve", evv, writes=[rbk, rvp])
                        blocks = []
                        for kbi in range(32):
                            r = kbi // nb
                            jl = kbi % nb
                            lo_l = max(128 * jl - 64, 0)
                            hi_l = min(128 * jl + 192, L)
                            c0 = lo_l - (128 * jl - 64)
                            blocks.append((kbi, r * L + lo_l, r * L + hi_l, c0))
                        plist = []
                        last_kb = {}
                        for (kbi, ulo, uhi, c0) in blocks:
                            ps = []
                            u = ulo
                            while u < uhi:
                                ue = min(uhi, (u // 512 + 1) * 512)
                                ps.append((u, ue))
                                last_kb[u // 512] = kbi
                                u = ue
                            plist.append(ps)
                        started = set()
                        sbank = {}

                        def qk(kbi):
                            (_, ulo, uhi, c0) = blocks[kbi]
                            n = uhi - ulo
                            for hh in range(2):
                                bk, rbk = next_bank(rot)

                                def mm(e, bk=bk, hh=hh, kbi=kbi, ulo=ulo, uhi=uhi, c0=c0, n=n, g=g, p=p):
                                    e.matmul(bk[:, 0:n], lhsT=kt[:, hh, 128 * kbi:128 * (kbi + 1)], rhs=qt[:, ulo:uhi],
                                             start=True, stop=False)
                                    return e.matmul(bk[:, 0:n], lhsT=ident[:, :], rhs=biasT[:, g * 8 + 2 * p + hh, c0:c0 + n],
                                                    start=False, stop=True)
                                S.op("pe", mm, reads=[rkt, rqt, r_ident, rbias], writes=[rbk])
                                pt, rpt = pr.next()
                                S.op("act", lambda e, bk=bk, pt=pt, n=n: e.activation(out=pt[:, 0:n], in_=bk[:, 0:n], func=AF.Exp),
                                     writes=[rbk, rpt])
                                sbank[(kbi, hh)] = (pt, rpt)

                        def pv(kbi):
                            (_, ulo, uhi, c0) = blocks[kbi]
                            for hh in range(2):
                                pt, rpt = sbank.pop((kbi, hh))
                                for (ua, ub) in plist[kbi]:
                                    m = ua // 512
                                    first = (m not in started)
                                    started.add(m)
                                    last = (last_kb[m] == kbi and hh == 1 and ub == max(x[1] for x in plist[kbi] if x[0] // 512 == m))
                                    nbk, dbk = banks[m % 2], banks[2 + m % 2]

                                    def mm(e, pt=pt, hh=hh, kbi=kbi, ua=ua, ub=ub, ulo=ulo, first=first, last=last, nbk=nbk, dbk=dbk):
                                        e.matmul(nbk[:, ua % 512:ua % 512 + ub - ua], lhsT=vp[:, kbi, hh, :], rhs=pt[:, ua - ulo:ub - ulo],
                                                 start=first, stop=last, skip_group_check=True)
                                        return e.matmul(dbk[:, ua % 512:ua % 512 + ub - ua], lhsT=e_h[:, hh, :], rhs=pt[:, ua - ulo:ub - ulo],
                                                        start=first, stop=last, skip_group_check=True)
                                    S.op("pe", mm, reads=[rvp, rpt, reh], writes=[r_bank[m % 2], r_bank[2 + m % 2]])
                            for m in sorted(set(x[0] // 512 for x in plist[kbi])):
                                if last_kb[m] != kbi:
                                    continue
                                if L >= 512:
                                    r = (512 * m) // L
                                    l0 = (512 * m) % L

                                    def view(t, r=r, l0=l0, d=d):
                                        return t[:, :].rearrange("p (l r) -> p r l", r=d)[:, r, l0:l0 + 512]

                                    def bview(b):
                                        return b[:, :]
                                else:
                                    nr = 512 // L
                                    r0 = m * nr

                                    def view(t, r0=r0, nr=nr, d=d):
                                        return t[:, :].rearrange("p (l r) -> p r l", r=d)[:, r0:r0 + nr, :]

                                    def bview(b, L=L):
                                        return b[:, :].rearrange("p (r l) -> p r l", l=L)
                                nbk, dbk = banks[m % 2], banks[2 + m % 2]
                                if g == 0:
                                    S.op("act", lambda e, nbk=nbk, view=view, bview=bview: e.activation(
                                        out=view(acc_n), in_=bview(nbk), func=AF.Copy), writes=[r_bank[m % 2], racc])
                                    S.op("dve", lambda e, dbk=dbk, view=view, bview=bview: e.tensor_copy(
                                        out=view(acc_d), in_=bview(dbk)), writes=[r_bank[2 + m % 2], racc])
                                else:
                                    def ad(e, nbk=nbk, dbk=dbk, view=view, bview=bview):
                                        e.tensor_tensor(out=view(acc_n), in0=bview(nbk), in1=view(acc_n), op=ALU.add)
                                        return e.tensor_tensor(out=view(acc_d), in0=bview(dbk), in1=view(acc_d), op=ALU.add)
                                    S.op("dve", ad, writes=[r_bank[m % 2], r_bank[2 + m % 2], racc])
                        qk(0)
                        qk(1)
                        for kbi in range(32):
                            pv(kbi)
                            if kbi + 2 < 32:
                                qk(kbi + 2)

                    S.op("act", lambda e: e.activation(out=acc_d[:, :], in_=acc_d[:, :], func=AF.Ln), writes=[racc])
                    S.op("act", lambda e: e.activation(out=acc_d[:, :], in_=acc_d[:, :], func=AF.Exp, scale=-1.0), writes=[racc])
                    S.op("dve", lambda e: e.tensor_tensor(out=ob[:, :], in0=acc_n[:, :], in1=acc_d[:, :], op=ALU.mult),
                         reads=[racc], writes=[rob])
                    S.op("pool", lambda e, p=p: e.dma_start(out=ot_d[p * 128:(p + 1) * 128, :], in_=ob[:, :]),
                         reads=[rob], writes=[r_out], dma=True)
                S.barrier(dummy)

        if stop_after == "A":
            phase_A(out)
        elif stop_after == "B":
            phase_A(x1_d)
            phase_ffn(0, x1_d, out)
        else:
            cast_gate[0] = phase_A(x1_d)
            late_casts()
            phase_ffn(0, x1_d, x2_d, hn2d=hn_d)
            phase_C()
            phase_ffn(1, x2_d, out, attn=True, final=True, mid2d=x3_d)
        S.op("sp", lambda e: e.nop(), reads=[r_out])
        S.emit()
    return nc


def _t5_bucket(rel):
    half = 16
    max_exact = 8
    n = np.abs(rel)
    side = np.where(rel > 0, half, 0)
    nf = np.maximum(n, 1).astype(np.float32)
    large = max_exact + (np.log(nf / max_exact) / np.log(1024 / max_exact) * (half - max_exact)).astype(np.int32)
    large = np.minimum(large, half - 1)
    return side + np.where(n < max_exact, n, large)


def _bias_tiles(rel_bias):
    r = np.arange(128)[:, None]
    c = np.arange(256)[None, :]
    j = r - c + 64
    valid = (j >= -64) & (j <= 64)
    out = np.empty((128, 24, 256), np.float32)
    for g in range(3):
        bid = _t5_bucket(DIL[g] * np.clip(j, -64, 64))
        for h in range(8):
            out[:, g * 8 + h, :] = np.where(valid, rel_bias[bid, g * 8 + h], np.float32(NEG))
    return out


def _pack_vecs(norm_w, final_norm, conv_w, ffn_conv_w, ffn_conv_b):
    cols = []
    for i in range(2):
        for j in range(2):
            cols.append(norm_w[i, j].reshape(KC, 128).T)
    cols.append(final_norm.reshape(KC, 128).T)
    for tap in range(3):
        cols.append(conv_w[0, tap].reshape(KC, 128).T)
    for i in range(2):
        for tap in range(3):
            cols.append(ffn_conv_w[i, tap].reshape(FCH, 128).T)
    for i in range(2):
        cols.append(ffn_conv_b[i].reshape(FCH, 128).T)
    v = np.ascontiguousarray(np.concatenate(cols, axis=1), dtype=np.float32)
    assert v.shape == (128, NV)
    return v


_PROG = {}


def kernel(x, norm_w, conv_in, conv_w, conv_out, attn_qkv, attn_out, rel_bias,
           ffn_up, ffn_conv_w, ffn_conv_b, ffn_down, final_norm, _stop_after="all", _cores=None):
    f = lambda a: np.asarray(a, dtype=np.float32)
    x = f(x)
    cores = list(range(8)) if _cores is None else list(_cores)
    shared = {
        "vecs": _pack_vecs(f(norm_w), f(final_norm), f(conv_w), f(ffn_conv_w), f(ffn_conv_b)),
        "ident": np.eye(128, dtype=np.float32),
        "biasT": _bias_tiles(f(rel_bias)),
        "conv_in": np.ascontiguousarray(f(conv_in)[0]),
        "conv_out": np.ascontiguousarray(f(conv_out)[0]),
        "attn_qkv": np.ascontiguousarray(f(attn_qkv)[0]),
        "attn_out": np.ascontiguousarray(f(attn_out)[0]),
        "ffn_up": np.ascontiguousarray(f(ffn_up)),
        "ffn_down": np.ascontiguousarray(f(ffn_down)),
    }
    in_maps = []
    for b in cores:
        m = dict(shared)
        m["xT"] = np.ascontiguousarray(x[b].T)
        in_maps.append(m)
    if _stop_after not in _PROG:
        _PROG[_stop_after] = build_program(_stop_after)
    res = run_bass_kernel_spmd(_PROG[_stop_after], in_maps, core_ids=list(range(len(cores))))
    outs = [np.ascontiguousarray(r["out"].T) for r in res.results]
    return np.stack(outs, axis=0).astype(np.float32)
```

```python
import contextlib
import numpy as np
import concourse.bass as bass
import concourse.mybir as mybir
from concourse.bass_utils import run_bass_kernel_spmd

F32 = mybir.dt.float32
BF16 = mybir.dt.bfloat16
AF = mybir.ActivationFunctionType
ALU = mybir.AluOpType

D = 1024
SEQ = 4096
FF = 2816
KC = 8
FCH = 22
EPS = 1e-6
NEG = -30000.0
DIL = (1, 4, 16)
NV = 240


class Res:
    __slots__ = ("name", "w", "r")

    def __init__(self, name=""):
        self.name = name
        self.w = None
        self.r = []


class Op:
    __slots__ = ("eng", "fn", "deps", "dma", "idx", "sig", "needs_inc")

    def __init__(self, eng, fn, dma):
        self.eng = eng
        self.fn = fn
        self.dma = dma
        self.deps = []
        self.idx = -1
        self.sig = None
        self.needs_inc = False


class Sched:
    ENGS = ("pe", "act", "dve", "pool", "sp")
    HANDLE = {"pe": "tensor", "act": "scalar", "dve": "vector", "pool": "gpsimd", "sp": "sync"}
    SEM_CAP = 20000

    def __init__(self, nc, n_dma_sems=64):
        self.nc = nc
        self.ops = {e: [] for e in self.ENGS}
        self.n_dma_sems = n_dma_sems
        self.dma_last = [None] * n_dma_sems
        self.dma_rr = {False: 0, True: 0}
        self.dma_order = []
        self.pending_dma = []
        self.pending_barrier = {}

    def op(self, eng, fn, reads=(), writes=(), dma=False, extra_deps=()):
        o = Op(eng, fn, dma)
        deps = []
        seen = set()

        def add(d):
            if d is not None and id(d) not in seen:
                seen.add(id(d))
                deps.append(d)
        for r in reads:
            add(r.w)
        for r in writes:
            add(r.w)
            for x in r.r:
                add(x)
        for d in extra_deps:
            add(d)
        if eng in self.pending_barrier:
            add(self.pending_barrier.pop(eng))
        if dma:
            sw = (eng == "pool")
            half = self.n_dma_sems // 2
            k = self.dma_rr[sw] + (half if sw else 0)
            self.dma_rr[sw] = (self.dma_rr[sw] + 1) % half
            add(self.dma_last[k])
            self.dma_last[k] = o
            o.sig = k
            self.dma_order.append(o)
            self.pending_dma.append(o)
        for r in reads:
            r.r.append(o)
        for r in writes:
            r.w = o
            r.r = []
        o.idx = len(self.ops[eng])
        pr = []
        for d in deps:
            if (not d.dma) and d.eng == eng:
                if eng == "pe":
                    continue
            pr.append(d)
        o.deps = pr
        for d in pr:
            d.needs_inc = True
        self.ops[eng].append(o)
        return o

    def barrier(self, dummy):
        deps = [self.ops[e][-1] for e in self.ENGS if self.ops[e]] + list(self.pending_dma)
        self.pending_dma = []
        j = self.op("dve", lambda e: e.memset(dummy[:, 0:1], 0.0), extra_deps=deps)
        self.pending_barrier = {e: j for e in self.ENGS if e != "dve"}
        return j

    def emit(self):
        nc = self.nc
        with contextlib.ExitStack() as st:
            for e in self.ENGS:
                n = sum(1 for o in self.ops[e] if o.needs_inc and not o.dma)
                nsem = max(1, (n + self.SEM_CAP - 1) // self.SEM_CAP)
                sems = [st.enter_context(nc.semaphore(f"s_{e}{i}")) for i in range(nsem)]
                c = 0
                for o in self.ops[e]:
                    if o.needs_inc and not o.dma:
                        o.sig = (sems[c // self.SEM_CAP], c % self.SEM_CAP + 1)
                        c += 1
            dsems = [st.enter_context(nc.semaphore(f"s_dma{i}")) for i in range(self.n_dma_sems)]
            dcount = [0] * self.n_dma_sems
            for o in self.dma_order:
                k = o.sig
                dcount[k] += 16
                o.sig = (dsems[k], dcount[k])
            block = st.enter_context(nc.Block())
            sched = self

            def make(e):
                def body(engh):
                    waited = {}
                    for o in sched.ops[e]:
                        for d in o.deps:
                            sem, val = d.sig
                            key = id(sem)
                            if waited.get(key, 0) < val:
                                engh.wait_ge(sem, val)
                                waited[key] = val
                        inst = o.fn(engh)
                        if o.dma:
                            inst.then_inc(o.sig[0], 16)
                        elif o.needs_inc:
                            inst.then_inc(o.sig[0], 1)
                return body
            for e in self.ENGS:
                if self.ops[e]:
                    getattr(block, self.HANDLE[e])(make(e))


class Ring:
    def __init__(self, tiles):
        self.tiles = tiles
        self.res = [Res() for _ in tiles]
        self.i = 0

    def next(self):
        k = self.i % len(self.tiles)
        self.i += 1
        return self.tiles[k], self.res[k]


def split_sizes(total, n):
    base = total // n
    rem = total - base * n
    return [base + (1 if i < rem else 0) for i in range(n)]


def pieces_of(total, maxn=512):
    n = (total + maxn - 1) // maxn
    out = []
    a = 0
    for s in split_sizes(total, n):
        out.append((a, s))
        a += s
    return out


def build_program(stop_after="all"):
    nc = bass.Bass("TRN2", target_bir_lowering=False)
    S = Sched(nc)

    def din(name, shape, dt=F32):
        return nc.dram_tensor(name, shape, dt, kind="ExternalInput").ap()

    def dscr(name, shape, dt):
        return nc.dram_tensor(name, shape, dt, kind="Internal").ap()

    xT = din("xT", [D, SEQ])
    vecs_d = din("vecs", [128, NV])
    ident_d = din("ident", [128, 128])
    biasT_d = din("biasT", [128, 24, 256])
    w_ci = din("conv_in", [D, 3 * D])
    w_co = din("conv_out", [D, D])
    w_qkv = din("attn_qkv", [D, 4608])
    w_ao = din("attn_out", [512, D])
    w_up = din("ffn_up", [2, D, 2 * FF])
    w_dn = din("ffn_down", [2, FF, D])
    out = nc.dram_tensor("out", [D, SEQ], F32, kind="ExternalOutput").ap()

    wb_ci = dscr("wb_ci", [D, 3 * D], BF16)
    wb_co = dscr("wb_co", [D, D], BF16)
    wb_qkv = dscr("wb_qkv", [D, 4608], BF16)
    wb_ao = dscr("wb_ao", [512, D], BF16)
    wb_up = [dscr(f"wb_up{i}", [D, 2 * FF], BF16) for i in range(2)]
    wb_dn = [dscr(f"wb_dn{i}", [4, 128, FCH, 256], BF16) for i in range(2)]
    x1_d = dscr("x1_d", [D, SEQ], F32)
    x2_d = dscr("x2_d", [D, SEQ], F32)
    x3_d = dscr("x3_d", [D, SEQ], F32)
    ot_d = dscr("ot_d", [512, SEQ], BF16)
    hn_d = dscr("hn_d", [8, 128, KC, 512], BF16)

    r_wci = [Res(), Res(), Res()]
    r_wco, r_wao = Res(), Res()
    r_wqkv = [Res(), Res(), Res()]
    r_wup = [[Res(), Res()], [Res(), Res()]]
    r_wdn = [[Res() for _ in range(4)] for _ in range(2)]
    r_out = Res()

    def fm(ap2d, lo, hi):
        return ap2d[:, lo:hi].rearrange("(c p) t -> p c t", p=128)

    with contextlib.ExitStack() as gst:
        def sb(name, shape, dt, st=gst):
            return st.enter_context(nc.sbuf_tensor(name, shape, dt))

        banks = [gst.enter_context(nc.psum_tensor(f"bank{i}", [128, 512], F32)) for i in range(8)]
        r_bank = [Res(f"bank{i}") for i in range(8)]
        bank_rr = [0]

        def next_bank(pool=None):
            pool = pool or list(range(8))
            k = pool[bank_rr[0] % len(pool)]
            bank_rr[0] += 1
            return banks[k], r_bank[k]

        vecs = sb("vecs_sb", [128, NV], F32)
        ident_f = sb("ident_f", [128, 128], F32)
        ident = sb("ident_sb", [128, 128], BF16)
        ones = sb("ones_sb", [128, 128], BF16)
        dummy = sb("dummy_sb", [128, 4], F32)
        r_vecs, r_identf, r_ident, r_ones = Res(), Res(), Res(), Res()
        S.op("sp", lambda e: e.dma_start(out=vecs[:], in_=vecs_d[:, :]), writes=[r_vecs], dma=True)
        S.op("sp", lambda e: e.dma_start(out=ident_f[:], in_=ident_d[:, :]), writes=[r_identf], dma=True)
        S.op("dve", lambda e: e.tensor_copy(out=ident[:], in_=ident_f[:]), reads=[r_identf], writes=[r_ident])
        S.op("dve", lambda e: e.memset(ones[:], 1.0), writes=[r_ones])

        def vcol(j):
            return vecs[:, j:j + 1]

        cast_prev = [None]
        cast_gate = [None]

        def cast(dst, src, res, chain=True):
            deps = [d for d in ((cast_prev[0] if chain else None), cast_gate[0]) if d is not None]
            cast_prev[0] = S.op("pool", lambda e: e.dma_start(out=dst, in_=src, max_dma_last_dim=4096), writes=[res], dma=True,
                                extra_deps=deps)
        for j3 in range(3):
            cast(wb_ci[:, j3 * D:(j3 + 1) * D], w_ci[:, j3 * D:(j3 + 1) * D], r_wci[j3])
        cast(wb_co[:, :], w_co[:, :], r_wco)

        def cast_ffn(i):
            up = wb_up[i].rearrange("r (k four m) -> r k four m", four=4, m=128)
            cast(up[:, :, 0:2, :], w_up[i, :, 0:FF].rearrange("r (k two m) -> r k two m", two=2, m=128), r_wup[i][0])
            cast(up[:, :, 2:4, :], w_up[i, :, FF:2 * FF].rearrange("r (k two m) -> r k two m", two=2, m=128), r_wup[i][1])
            for og in range(4):
                cast(wb_dn[i][og, :, :, :], w_dn[i, :, og * 256:(og + 1) * 256].rearrange("(fc p) m -> p fc m", p=128), r_wdn[i][og])
        cast_ffn(0)

        def late_casts():
            qv = wb_qkv.rearrange("r (gp j m) -> r gp j m", j=3, m=128)
            for j in range(3):
                cast(qv[:, :, j, :], w_qkv[:, j * 1536:(j + 1) * 1536].rearrange("r (gp m) -> r gp m", m=128), r_wqkv[j])
            cast(wb_ao[:, :], w_ao[:, :], r_wao)
            cast_ffn(1)


        def load_window(dst, rdst, src2d, t0, ncols, eng="sp", mem_eng="pool"):
            lo = max(t0, 0)
            hi = min(t0 + ncols, SEQ)
            if lo > t0:
                S.op(mem_eng, lambda e: e.memset(dst[:, :, 0:lo - t0], 0.0), writes=[rdst])
            if hi < t0 + ncols:
                S.op(mem_eng, lambda e: e.memset(dst[:, :, hi - t0:ncols], 0.0), writes=[rdst])
            S.op(eng, lambda e: e.dma_start(out=dst[:, :, lo - t0:hi - t0], in_=fm(src2d, lo, hi)),
                 writes=[rdst], dma=True)

        def rms_sq(xw, rxw, sq, rsq, ncols):
            S.op("act", lambda e: e.activation(out=sq[:, :, 0:ncols], in_=xw[:, :, 0:ncols], func=AF.Square),
                 reads=[rxw], writes=[rsq])

        def rms_fin(sq, rsq, rstd, rrstd, ncols, roff=0):
            bk, rbk = next_bank()

            def mm(e):
                for c in range(KC):
                    i = e.matmul(bk[:, 0:ncols], lhsT=ones[:, :], rhs=sq[:, c, 0:ncols], start=(c == 0), stop=(c == KC - 1))
                return i
            S.op("pe", mm, reads=[rsq, r_ones], writes=[rbk])
            S.op("act", lambda e: e.activation(out=rstd[:, roff:roff + ncols], in_=bk[:, 0:ncols], func=AF.Ln,
                                               scale=1.0 / D, bias=EPS), writes=[rbk, rrstd])
            S.op("act", lambda e: e.activation(out=rstd[:, roff:roff + ncols], in_=rstd[:, roff:roff + ncols],
                                               func=AF.Exp, scale=-0.5), writes=[rrstd])

        def rms_stats(xw, rxw, sq, rsq, rstd, rrstd, ncols, roff=0):
            rms_sq(xw, rxw, sq, rsq, ncols)
            rms_fin(sq, rsq, rstd, rrstd, ncols, roff)

        def rms_apply(xw, rxw, rstd, rrstd, xn, rxn, ncols, wcol, roff=0, xoff=0):
            def f(e):
                for c in range(KC):
                    i = e.scalar_tensor_tensor(out=xn[:, c, xoff:xoff + ncols], in0=xw[:, c, 0:ncols], scalar=vcol(wcol + c),
                                               in1=rstd[:, roff:roff + ncols], op0=ALU.mult, op1=ALU.mult)
                return i
            S.op("dve", f, reads=[rxw, rrstd, r_vecs], writes=[rxn])

        def make_diag(diag, rdiag, ncol, colbase):
            def f(e):
                for j in range(ncol):
                    i = e.tensor_scalar(out=diag[:, j, :], in0=ident_f[:, :], scalar1=vcol(colbase + j), scalar2=None, op0=ALU.mult)
                return i
            S.op("dve", f, reads=[r_identf, r_vecs], writes=[rdiag])

        def phase_A(dst2d):
            with contextlib.ExitStack() as st:
                WM = 458
                xr = Ring([sb(f"A_x{i}", [128, KC, WM], F32, st) for i in range(2)])
                sq = sb("A_sq", [128, KC, WM], BF16, st)
                rsq = Res()
                rstd = sb("A_rstd", [128, WM], F32, st)
                rrstd = Res()
                xn = sb("A_xn", [128, KC, WM], BF16, st)
                rxn = Res()
                b_sb = sb("A_b", [128, KC, WM], BF16, st)
                rb = [Res() for _ in range(KC)]
                cr = Ring([sb(f"A_c{i}", [128, WM], F32, st) for i in range(KC)])
                z = sb("A_z", [128, KC, WM], BF16, st)
                rz = [Res() for _ in range(KC)]
                y = sb("A_y", [128, KC, WM], BF16, st)
                ry = [Res() for _ in range(KC)]
                wci_sb = sb("A_wci", [128, KC, 3 * D], BF16, st)
                rwci_sb = [[Res() for _ in range(KC)] for _ in range(3)]

                def wci_load(j3):
                    for kc in range(KC):
                        S.op("sp", lambda e, j3=j3, kc=kc: e.dma_start(
                            out=wci_sb[:, kc, j3 * D:(j3 + 1) * D],
                            in_=wb_ci[kc * 128:(kc + 1) * 128, j3 * D:(j3 + 1) * D]),
                            reads=[r_wci[j3]], writes=[rwci_sb[j3][kc]], dma=True)
                wco = sb("A_wco", [128, KC, D], BF16, st)
                rwco = Res()
                diag = sb("A_diag", [128, 24, 128], BF16, st)
                rdiag = Res()
                make_diag(diag, rdiag, 24, 40)
                sizes = split_sizes(SEQ, 9)
                tstarts = [sum(sizes[:i]) for i in range(9)]
                xn2 = [xn, sb("A_xn2", [128, KC, WM], BF16, st)]
                rxn2 = [Res(), Res()]
                xws = {}

                def pro_a(i):
                    xw, rxw = xr.next()
                    xws[i] = (xw, rxw)
                    load_window(xw, rxw, xT, tstarts[i] - 1, sizes[i] + 2, mem_eng=("dve" if i == 0 else "pool"))
                    rms_sq(xw, rxw, sq, rsq, sizes[i] + 2)

                def pro_b(i):
                    xw, rxw = xws[i]
                    rms_fin(sq, rsq, rstd, rrstd, sizes[i] + 2)
                    rms_apply(xw, rxw, rstd, rrstd, xn2[i % 2], rxn2[i % 2], sizes[i] + 2, 0)
                pro_a(0)
                wci_load(0)
                pro_b(0)
                wci_load(1)
                wci_load(2)
                pending = []
                for ti, T in enumerate(sizes):
                    t0 = tstarts[ti]
                    W = T + 2
                    xw, rxw = xws[ti]
                    xn, rxn = xn2[ti % 2], rxn2[ti % 2]
                    while pending:
                        pending.pop()()
                    def inproj(col0, rw):
                        bk, rbk = next_bank()

                        def mm(e, bk=bk, col0=col0, W=W, xn=xn):
                            for c in range(KC):
                                i = e.matmul(bk[:, 0:W], lhsT=wci_sb[:, c, col0:col0 + 128], rhs=xn[:, c, 0:W],
                                             start=(c == 0), stop=(c == KC - 1))
                            return i
                        S.op("pe", mm, reads=list(rw) + [rxn], writes=[rbk])
                        return bk, rbk
                    for k in range(KC):
                        bk, rbk = inproj(k * 128, rwci_sb[0])
                        S.op("act", lambda e, bk=bk, k=k, W=W: e.activation(out=b_sb[:, k, 0:W], in_=bk[:, 0:W], func=AF.Copy),
                             writes=[rbk, rb[k]])
                    cts = []
                    for k in range(KC):
                        bk, rbk = inproj(D + k * 128, rwci_sb[1])
                        ct, rct = cr.next()
                        cts.append((ct, rct))
                        S.op("act", lambda e, bk=bk, ct=ct, W=W: e.activation(out=ct[:, 0:W], in_=bk[:, 0:W], func=AF.Copy),
                             writes=[rbk, rct])
                    for k in range(KC):
                        ct, rct = cts[k]
                        bk, rbk = inproj(2 * D + k * 128, rwci_sb[2])
                        S.op("dve", lambda e, bk=bk, ct=ct, k=k, W=W: e.tensor_tensor(
                            out=z[:, k, 0:W], in0=bk[:, 0:W], in1=ct[:, 0:W], op=ALU.mult),
                            reads=[rct], writes=[rbk, rz[k]])
                    if ti == 0:
                        S.op("sp", lambda e: e.dma_start(out=wco[:], in_=wb_co.rearrange("(c p) n -> p c n", p=128)),
                             reads=[r_wco], writes=[rwco], dma=True)
                    while pending:
                        pending.pop()()
                    if ti + 1 < 9:
                        pro_a(ti + 1)
                    for k in range(KC):
                        bk, rbk = next_bank()

                        def cv(e, bk=bk, k=k, T=T):
                            for tap in range(3):
                                i = e.matmul(bk[:, 0:T], lhsT=diag[:, tap * 8 + k, :], rhs=z[:, k, tap:tap + T],
                                             start=(tap == 0), stop=(tap == 2))
                            return i
                        S.op("pe", cv, reads=[rdiag, rz[k]], writes=[rbk])
                        S.op("dve", lambda e, bk=bk, k=k, T=T: e.tensor_tensor(
                            out=y[:, k, 0:T], in0=bk[:, 0:T], in1=b_sb[:, k, 1:T + 1], op=ALU.mult),
                            reads=[rb[k]], writes=[rbk, ry[k]])
                    if ti + 1 < 9:
                        pro_b(ti + 1)
                    for oc in range(KC):
                        bk, rbk = next_bank()

                        def mm(e, bk=bk, oc=oc, T=T):
                            for c in range(KC):
                                i = e.matmul(bk[:, 0:T], lhsT=wco[:, c, oc * 128:(oc + 1) * 128], rhs=y[:, c, 0:T],
                                             start=(c == 0), stop=(c == KC - 1))
                            return i
                        S.op("pe", mm, reads=[rwco] + ry, writes=[rbk])
                        S.op("dve", lambda e, bk=bk, oc=oc, xw=xw, T=T: e.tensor_tensor(
                            out=xw[:, oc, 1:T + 1], in0=bk[:, 0:T], in1=xw[:, oc, 1:T + 1], op=ALU.add),
                            writes=[rbk, rxw])
                    pending.append(lambda xw=xw, rxw=rxw, t0=t0, T=T: S.op(
                        "act", lambda e: e.dma_start(out=fm(dst2d, t0, t0 + T), in_=xw[:, :, 1:T + 1]),
                        reads=[rxw], writes=[r_out], dma=True))
                while pending:
                    pending.pop()()
                return S.barrier(dummy)

        def phase_ffn(li, src2d, dst2d, attn=False, final=False, mid2d=None, hn2d=None):
            with contextlib.ExitStack() as st:
                nmt = 3
                msizes = split_sizes(SEQ, nmt)
                starts = [sum(msizes[:i]) for i in range(nmt)]
                TM = max(msizes)
                WM = TM + 2
                PM = 458
                xr = Ring([sb(f"F{li}_x{i}", [128, KC, PM], F32, st) for i in range(2)])
                xp0 = sb(f"F{li}_xp", [128, KC, PM], F32, st)
                rxp0 = Res()
                sq = sb(f"F{li}_sq", [128, KC, PM], BF16, st)
                rsq = Res()
                rstd = sb(f"F{li}_rstd", [128, WM], F32, st)
                rrstd = [Res() for _ in range(4)]
                rstd_e = sb(f"F{li}_rstde", [128, PM], F32, st)
                xn = sb(f"F{li}_xn", [128, KC, WM], BF16, st)
                rxn = [Res() for _ in range(4)]
                wr = Ring([sb(f"F{li}_w{i}", [128, KC, 512], BF16, st) for i in range(2)])
                wdr = Ring([sb(f"F{li}_wd{i}", [128, FCH, 256], BF16, st) for i in range(2)])
                a = sb(f"F{li}_a", [128, FCH, TM], BF16, st)
                ra = [Res() for _ in range(FCH)]
                gr = Ring([sb(f"F{li}_g{i}", [128, WM], BF16, st) for i in range(2)])
                sr = Ring([sb(f"F{li}_s{i}", [128, PM], BF16, st) for i in range(3)])
                t2r = Ring([sb(f"F{li}_t{i}", [128, PM], F32, st) for i in range(2)])
                dr = Ring([sb(f"F{li}_dg{i}", [128, 3, 128], BF16, st) for i in range(3)])
                r_mid = []
                if attn:
                    wao = sb(f"F{li}_wao", [128, 4, D], BF16, st)
                    rwao = Res()
                    S.op("sp", lambda e: e.dma_start(out=wao[:], in_=wb_ao.rearrange("(c p) n -> p c n", p=128)),
                         reads=[r_wao], writes=[rwao], dma=True)
                    otr = Ring([sb(f"F{li}_ot{i}", [128, 4, PM], BF16, st) for i in range(2)])
                nwcol = (li * 2 + 1) * 8
                fcw = 64 + li * 3 * FCH
                fcb = 196 + li * FCH

                def xn_res(wp, lo, hi):
                    return [rxn[i] for i, (a0, n) in enumerate(wp) if a0 < hi and a0 + n > lo]

                xbuf = {}

                def pro_load(mi, pi, buf=None):
                    t0, T = starts[mi], msizes[mi]
                    (a0, n) = pieces_of(T + 2)[pi]
                    xp, rxp = buf if buf is not None else (xp0, rxp0)
                    xbuf[(mi, pi)] = (xp, rxp)
                    load_window(xp, rxp, src2d, t0 - 1 + a0, n, mem_eng=("dve" if mi == 0 else "pool"))

                def pro_a(mi, pi, buf=None):
                    t0, T = starts[mi], msizes[mi]
                    (a0, n) = pieces_of(T + 2)[pi]
                    if (mi, pi) not in xbuf:
                        pro_load(mi, pi, buf)
                    xp, rxp = xbuf[(mi, pi)]
                    if attn:
                        ot, rot = otr.next()
                        lo = max(t0 - 1 + a0, 0)
                        hi = min(t0 - 1 + a0 + n, SEQ)
                        o0 = lo - (t0 - 1 + a0)
                        if o0 > 0:
                            S.op("dve" if mi == 0 else "pool", lambda e, ot=ot, o0=o0: e.memset(ot[:, :, 0:o0], 0.0), writes=[rot])
                        if hi - (t0 - 1 + a0) < n:
                            S.op("dve" if mi == 0 else "pool", lambda e, ot=ot, h=hi - (t0 - 1 + a0), n=n: e.memset(ot[:, :, h:n], 0.0), writes=[rot])
                        S.op("sp", lambda e, ot=ot, lo=lo, hi=hi, o0=o0: e.dma_start(
                            out=ot[:, :, o0:o0 + hi - lo], in_=ot_d[:, lo:hi].rearrange("(c p) t -> p c t", p=128)),
                            writes=[rot], dma=True)
                        for oc in range(KC):
                            bk, rbk = next_bank()

                            def mm(e, bk=bk, oc=oc, ot=ot, n=n):
                                for c in range(4):
                                    i = e.matmul(bk[:, 0:n], lhsT=wao[:, c, oc * 128:(oc + 1) * 128], rhs=ot[:, c, 0:n],
                                                 start=(c == 0), stop=(c == 3))
                                return i
                            S.op("pe", mm, reads=[rwao, rot], writes=[rbk])
                            S.op("dve", lambda e, bk=bk, oc=oc, n=n, xp=xp: e.tensor_tensor(
                                out=xp[:, oc, 0:n], in0=bk[:, 0:n], in1=xp[:, oc, 0:n], op=ALU.add),
                                writes=[rbk, rxp])
                        m_lo = max(a0, 1)
                        m_hi = min(a0 + n, T + 1)
                        rm = Res()
                        r_mid.append(rm)
                        S.op("act", lambda e, m_lo=m_lo, m_hi=m_hi, a0=a0, t0=t0, xp=xp: e.dma_start(
                            out=fm(mid2d, t0 - 1 + m_lo, t0 - 1 + m_hi), in_=xp[:, :, m_lo - a0:m_hi - a0]),
                            reads=[rxp], writes=[rm], dma=True)
                    rms_sq(xp, rxp, sq, rsq, n)

                def pro_b(mi, pi):
                    T = msizes[mi]
                    (a0, n) = pieces_of(T + 2)[pi]
                    xp, rxp = xbuf[(mi, pi)]
                    rms_fin(sq, rsq, rstd, rrstd[pi], n, roff=a0)
                    rms_apply(xp, rxp, rstd, rrstd[pi], xn, rxn[pi], n, nwcol, roff=a0, xoff=a0)

                def up_proj(mi):
                    T = msizes[mi]
                    W = T + 2
                    wp = pieces_of(W)
                    mp = pieces_of(T)
                    for k in range(FCH // 2):
                        wt, rwt = wr.next()
                        S.op("sp", lambda e, wt=wt, k=k: e.dma_start(
                            out=wt[:], in_=wb_up[li][:, k * 512:(k + 1) * 512].rearrange("(c p) n -> p c n", p=128)),
                            reads=r_wup[li], writes=[rwt], dma=True)
                        if k == FCH // 2 - 1:
                            prefetch_down(mi)
                        for q in range(2):
                            f = 2 * k + q
                            gt, rgt = gr.next()
                            dg, rdg = dr.next()

                            def mk(e, dg=dg, f=f):
                                for tap in (0, 2):
                                    i = e.tensor_scalar(out=dg[:, tap, :], in0=ident_f[:, :], scalar1=vcol(fcw + tap * FCH + f),
                                                        scalar2=None, op0=ALU.mult)
                                return i
                            S.op("dve", mk, reads=[r_identf, r_vecs], writes=[rdg])
                            for pi, (a0, n) in enumerate(wp):
                                bk, rbk = next_bank()

                                def mm(e, bk=bk, wt=wt, q=q, a0=a0, n=n):
                                    for c in range(KC):
                                        i = e.matmul(bk[:, 0:n], lhsT=wt[:, c, q * 128:(q + 1) * 128], rhs=xn[:, c, a0:a0 + n],
                                                     start=(c == 0), stop=(c == KC - 1))
                                    return i
                                S.op("pe", mm, reads=[rwt, rxn[pi]], writes=[rbk])
                                S.op("dve", lambda e, bk=bk, gt=gt, a0=a0, n=n: e.tensor_copy(out=gt[:, a0:a0 + n], in_=bk[:, 0:n]),
                                     writes=[rbk, rgt])
                            ubanks = []
                            for (m0, n) in mp:
                                bk, rbk = next_bank()

                                def mm(e, bk=bk, wt=wt, q=q, m0=m0, n=n):
                                    for c in range(KC):
                                        i = e.matmul(bk[:, 0:n], lhsT=wt[:, c, (2 + q) * 128:(3 + q) * 128], rhs=xn[:, c, 1 + m0:1 + m0 + n],
                                                     start=(c == 0), stop=(c == KC - 1))
                                    return i
                                S.op("pe", mm, reads=[rwt] + xn_res(wp, 1 + m0, 1 + m0 + n), writes=[rbk])
                                ubanks.append((bk, rbk))
                            for pi, (m0, n) in enumerate(mp):
                                bk, rbk = next_bank()

                                def cv(e, bk=bk, dg=dg, gt=gt, m0=m0, n=n):
                                    for tap in (0, 2):
                                        i = e.matmul(bk[:, 0:n], lhsT=dg[:, tap, :], rhs=gt[:, m0 + tap:m0 + tap + n],
                                                     start=(tap == 0), stop=(tap == 2))
                                    return i
                                S.op("pe", cv, reads=[rdg, rgt], writes=[rbk])
                                t2, rt2 = t2r.next()
                                S.op("dve", lambda e, bk=bk, t2=t2, gt=gt, m0=m0, n=n, f=f: e.scalar_tensor_tensor(
                                    out=t2[:, 0:n], in0=gt[:, m0 + 1:m0 + 1 + n], scalar=vcol(fcw + FCH + f),
                                    in1=bk[:, 0:n], op0=ALU.mult, op1=ALU.add),
                                    reads=[rgt, r_vecs], writes=[rbk, rt2])
                                stt, rstt = sr.next()
                                S.op("act", lambda e, t2=t2, stt=stt, n=n, f=f: e.activation(
                                    out=stt[:, 0:n], in_=t2[:, 0:n], func=AF.Silu, bias=vcol(fcb + f)),
                                    reads=[r_vecs, rt2], writes=[rstt])
                                ub, rub = ubanks[pi]
                                S.op("dve", lambda e, ub=ub, stt=stt, f=f, m0=m0, n=n: e.tensor_tensor(
                                    out=a[:, f, m0:m0 + n], in0=ub[:, 0:n], in1=stt[:, 0:n], op=ALU.mult),
                                    reads=[rstt], writes=[rub, ra[f]])
                        if k == 0:
                            while deferred:
                                deferred.pop()()

                deferred = []

                pre_dn = {}

                def wd_load(og):
                    wd, rwd = wdr.next()
                    S.op("sp", lambda e, wd=wd, og=og: e.dma_start(out=wd[:], in_=wb_dn[li][og, :, :, :]),
                         reads=[r_wdn[li][og]], writes=[rwd], dma=True)
                    return wd, rwd

                def x_reload(xw, rxw, t0, m0, n):
                    S.op("sp", lambda e: e.dma_start(
                        out=xw[:, :, 0:n], in_=fm(mid2d if attn else src2d, t0 + m0, t0 + m0 + n)),
                        reads=list(r_mid), writes=[rxw], dma=True)

                def prefetch_down(mi):
                    (m0, n) = pieces_of(msizes[mi])[0]
                    xw, rxw = xr.next()
                    wds = {0: wd_load(0), 1: wd_load(1)}
                    x_reload(xw, rxw, starts[mi], m0, n)
                    pre_dn[mi] = (xw, rxw, wds)

                def down_proj(mi, nxt):
                    t0, T = starts[mi], msizes[mi]
                    mp = pieces_of(T)
                    for j, (m0, n) in enumerate(mp):
                        if nxt is not None and attn:
                            pro_a(nxt, j)
                        if j == 0 and mi in pre_dn:
                            xw, rxw, wds = pre_dn.pop(mi)
                        else:
                            xw, rxw = xr.next()
                            wds = {0: wd_load(0), 1: wd_load(1)}
                            x_reload(xw, rxw, t0, m0, n)
                        if nxt is not None and not attn:
                            pro_a(nxt, j)
                        for og in range(4):
                            if og >= 2:
                                wds[og] = wd_load(og)
                            wd, rwd = wds[og]
                            if j == 0 and og == 0:
                                CS = FCH - 2
                                grp = [next_bank(), next_bank()]
                                for q, (bk, rbk) in enumerate(grp):
                                    def mm1(e, bk=bk, wd=wd, q=q, m0=m0, n=n):
                                        for c in range(CS):
                                            i = e.matmul(bk[:, 0:n], lhsT=wd[:, c, q * 128:(q + 1) * 128], rhs=a[:, c, m0:m0 + n],
                                                         start=(c == 0), stop=False)
                                        return i
                                    S.op("pe", mm1, reads=[rwd] + ra[:CS], writes=[rbk])
                                for q, (bk, rbk) in enumerate(grp):
                                    oc = q

                                    def mm2(e, bk=bk, wd=wd, q=q, m0=m0, n=n):
                                        for c in range(CS, FCH):
                                            i = e.matmul(bk[:, 0:n], lhsT=wd[:, c, q * 128:(q + 1) * 128], rhs=a[:, c, m0:m0 + n],
                                                         start=False, stop=(c == FCH - 1))
                                        return i
                                    S.op("pe", mm2, reads=[rwd] + ra[CS:], writes=[rbk])
                                    S.op("dve", lambda e, bk=bk, oc=oc, xw=xw, n=n: e.tensor_tensor(
                                        out=xw[:, oc, 0:n], in0=bk[:, 0:n], in1=xw[:, oc, 0:n], op=ALU.add),
                                        writes=[rbk, rxw])
                                continue
                            for q in range(2):
                                oc = og * 2 + q
                                bk, rbk = next_bank()

                                def mm(e, bk=bk, wd=wd, q=q, m0=m0, n=n):
                                    for c in range(FCH):
                                        i = e.matmul(bk[:, 0:n], lhsT=wd[:, c, q * 128:(q + 1) * 128], rhs=a[:, c, m0:m0 + n],
                                                     start=(c == 0), stop=(c == FCH - 1))
                                    return i
                                S.op("pe", mm, reads=[rwd] + ra, writes=[rbk])
                                S.op("dve", lambda e, bk=bk, oc=oc, xw=xw, n=n: e.tensor_tensor(
                                    out=xw[:, oc, 0:n], in0=bk[:, 0:n], in1=xw[:, oc, 0:n], op=ALU.add),
                                    writes=[rbk, rxw])
                            if og == 1 and nxt is not None:
                                pro_b(nxt, j)
                            if og == 2 and deferred:
                                deferred.pop()()

                        def epilogue(xw=xw, rxw=rxw, m0=m0, n=n, t0=t0):
                            if not final:
                                S.op("act", lambda e: e.dma_start(out=fm(dst2d, t0 + m0, t0 + m0 + n), in_=xw[:, :, 0:n]),
                                     reads=[rxw], writes=[r_out], dma=True)
                            if final or hn2d is not None:
                                rms_stats(xw, rxw, sq, rsq, rstd_e, rrstd[3], n, roff=0)
                            if hn2d is not None:
                                def fh(e):
                                    for c in range(KC):
                                        i = e.scalar_tensor_tensor(out=sq[:, c, 0:n], in0=xw[:, c, 0:n], scalar=vcol(16 + c),
                                                                   in1=rstd_e[:, 0:n], op0=ALU.mult, op1=ALU.mult)
                                    return i
                                S.op("dve", fh, reads=[rxw, rrstd[3], r_vecs], writes=[rsq])
                                T0_, T1_ = t0 + m0, t0 + m0 + n
                                for n8 in range(T0_ // 512, (T1_ - 1) // 512 + 1):
                                    lo_, hi_ = max(T0_, 512 * n8), min(T1_, 512 * (n8 + 1))
                                    S.op("act", lambda e, n8=n8, lo_=lo_, hi_=hi_, T0_=T0_: e.dma_start(
                                        out=hn2d[n8, :, :, lo_ - 512 * n8:hi_ - 512 * n8], in_=sq[:, :, lo_ - T0_:hi_ - T0_]),
                                        reads=[rsq], writes=[Res()], dma=True)
                            if final:
                                def fo(e):
                                    for c in range(KC):
                                        i = e.scalar_tensor_tensor(out=xw[:, c, 0:n], in0=xw[:, c, 0:n], scalar=vcol(32 + c),
                                                                   in1=rstd_e[:, 0:n], op0=ALU.mult, op1=ALU.mult)
                                    return i
                                S.op("dve", fo, reads=[rrstd[3], r_vecs], writes=[rxw])
                            if final:
                                S.op("act", lambda e: e.dma_start(out=fm(dst2d, t0 + m0, t0 + m0 + n), in_=xw[:, :, 0:n]),
                                     reads=[rxw], writes=[r_out], dma=True)
                        deferred.append(epilogue)
                    if nxt is None:
                        while deferred:
                            deferred.pop()()

                init_bufs = [(xp0, rxp0), (xr.tiles[0], xr.res[0]), (xr.tiles[1], xr.res[1])]
                for pi in range(3):
                    pro_load(0, pi, init_bufs[pi])
                for pi in range(3):
                    pro_a(0, pi)
                    pro_b(0, pi)
                for mi in range(nmt):
                    up_proj(mi)
                    down_proj(mi, mi + 1 if mi + 1 < nmt else None)
                S.barrier(dummy)

        def phase_C():
            with contextlib.ExitStack() as st:
                hn = sb("C_hn", [128, KC, SEQ], BF16, st)
                rhn = Res()
                rstd = sb("C_rstd", [128, 512], F32, st)
                rrstd = Res()
                rhns = [Res() for _ in range(8)]
                def hn_load(n8, q):
                    S.op(q, lambda e, n8=n8: e.dma_start(out=hn[:, :, 512 * n8:512 * (n8 + 1)], in_=hn_d[n8, :, :, :]),
                         writes=[rhns[n8]], dma=True)
                hn_load(0, "act")
                biasT = sb("C_bias", [128, 24, 256], BF16, st)
                rbias = Res()
                S.op("pool", lambda e: e.dma_start(out=biasT[:], in_=biasT_d[:, :, :]), writes=[rbias], dma=True)
                e_h = sb("C_eh", [128, 2, 128], BF16, st)
                reh = Res()

                S.op("dve", lambda e: e.memset(e_h[:], 0.0), writes=[reh])
                S.op("dve", lambda e: e.memset(e_h[:, 0, 0:64], 1.0), writes=[reh])
                S.op("dve", lambda e: e.memset(e_h[:, 1, 64:128], 1.0), writes=[reh])
                qt = sb("C_qt", [128, SEQ], BF16, st)
                rqt = Res()
                kt = sb("C_kt", [128, 2, SEQ], BF16, st)
                rkt = Res()
                vp = sb("C_vp", [128, 32, 2, 128], BF16, st)
                rvp = Res()
                S.op("pool", lambda e: e.memset(kt[:], 0.0), writes=[rkt])
                S.op("pool", lambda e: e.memset(vp[:], 0.0), writes=[rvp])
                acc_n = sb("C_accn", [128, SEQ], F32, st)
                acc_d = sb("C_accd", [128, SEQ], F32, st)
                racc = Res()
                ob = sb("C_ob", [128, SEQ], BF16, st)
                rob = Res()
                wr = Ring([sb(f"C_w{i}", [128, KC, 384], BF16, st) for i in range(2)])
                pr = Ring([sb(f"C_p{i}", [128, 256], BF16, st) for i in range(8)])
                rot = [4, 5, 6, 7]
                for p in range(4):
                    for g in range(3):
                        d = DIL[g]
                        L = SEQ // d
                        nb = L // 128
                        gp = g * 4 + p
                        wt, rwt = wr.next()
                        S.op("sp", lambda e, wt=wt, gp=gp: e.dma_start(
                            out=wt[:], in_=wb_qkv[:, gp * 384:(gp + 1) * 384].rearrange("(c p) n -> p c n", p=128)),
                            reads=r_wqkv, writes=[rwt], dma=True)
                        if p == 0 and g == 0:
                            for n8 in range(1, 8):
                                hn_load(n8, "act" if n8 % 2 == 0 else "sp")
                        qv3 = qt[:, :].rearrange("p (r l) -> p r l", r=d)
                        for j in range(2):
                            for n in range(8):
                                bk, rbk = next_bank(rot)

                                def mm(e, bk=bk, wt=wt, j=j, n=n):
                                    for c in range(KC):
                                        i = e.matmul(bk[:, :], lhsT=wt[:, c, j * 128:(j + 1) * 128], rhs=hn[:, c, 512 * n:512 * (n + 1)],
                                                     start=(c == 0), stop=(c == KC - 1))
                                    return i
                                S.op("pe", mm, reads=[rwt, rhns[n]], writes=[rbk])
                                lw = 512 // d
                                src = bk[:, :].rearrange("p (l r) -> p r l", r=d)
                                if j == 0:
                                    S.op("act", lambda e, src=src, qv3=qv3, n=n, lw=lw: e.activation(
                                        out=qv3[:, :, lw * n:lw * (n + 1)], in_=src, func=AF.Copy, scale=0.125),
                                        writes=[rbk, rqt])
                                else:
                                    def ev(e, bk=bk, n=n, lw=lw, d=d):
                                        for hh in range(2):
                                            o = kt[hh * 64:(hh + 1) * 64, hh, :].rearrange("p (r l) -> p r l", r=d)
                                            s_ = bk[hh * 64:(hh + 1) * 64, :].rearrange("p (l r) -> p r l", r=d)
                                            i = e.tensor_copy(out=o[:, :, lw * n:lw * (n + 1)], in_=s_)
                                        return i
                                    S.op("dve", ev, writes=[rbk, rkt])
                        for b4 in range(8):
                            bk, rbk = next_bank(rot)

                            def mv(e, bk=bk, wt=wt, b4=b4, d=d, L=L):
                                for qq in range(4):
                                    kbi = b4 * 4 + qq
                                    r = (128 * kbi) // L
                                    l0 = (128 * kbi) % L
                                    s0 = r + d * l0
                                    for c in range(KC):
                                        i = e.matmul(bk[:, qq * 128:(qq + 1) * 128], lhsT=hn[:, c, s0:s0 + 127 * d + 1:d],
                                                     rhs=wt[:, c, 256:384], start=(c == 0), stop=(c == KC - 1))
                                return i
                            S.op("pe", mv, reads=[rwt] + rhns, writes=[rbk])

                            def evv(e, bk=bk, b4=b4):
                                s_ = bk[:, :].rearrange("p (q c) -> p q c", c=128)
                                e.tensor_copy(out=vp[:, 4 * b4:4 * b4 + 4, 0, 0:64], in_=s_[:, :, 0:64])
                                return e.tensor_copy(out=vp[:, 4 * b4:4 * b4 + 4, 1, 64:128], in_=s_[:, :, 64:128])
                            S.op("dve", evv, writes=[rbk, rvp])
                        blocks = []
                        for kbi in range(32):
                            r = kbi // nb
                            jl = kbi % nb
                            lo_l = max(128 * jl - 64, 0)
                            hi_l = min(128 * jl + 192, L)
                            c0 = lo_l - (128 * jl - 64)
                            blocks.append((kbi, r * L + lo_l, r * L + hi_l, c0))
                        plist = []
                        last_kb = {}
                        for (kbi, ulo, uhi, c0) in blocks:
                            ps = []
                            u = ulo
                            while u < uhi:
                                ue = min(uhi, (u // 512 + 1) * 512)
                                ps.append((u, ue))
                                last_kb[u // 512] = kbi
                                u = ue
                            plist.append(ps)
                        started = set()
                        sbank = {}

                        def qk(kbi):
                            (_, ulo, uhi, c0) = blocks[kbi]
                            n = uhi - ulo
                            for hh in range(2):
                                bk, rbk = next_bank(rot)

                                def mm(e, bk=bk, hh=hh, kbi=kbi, ulo=ulo, uhi=uhi, c0=c0, n=n, g=g, p=p):
                                    e.matmul(bk[:, 0:n], lhsT=kt[:, hh, 128 * kbi:128 * (kbi + 1)], rhs=qt[:, ulo:uhi],
                                             start=True, stop=False)
                                    return e.matmul(bk[:, 0:n], lhsT=ident[:, :], rhs=biasT[:, g * 8 + 2 * p + hh, c0:c0 + n],
                                                    start=False, stop=True)
                                S.op("pe", mm, reads=[rkt, rqt, r_ident, rbias], writes=[rbk])
                                pt, rpt = pr.next()
                                S.op("act", lambda e, bk=bk, pt=pt, n=n: e.activation(out=pt[:, 0:n], in_=bk[:, 0:n], func=AF.Exp),
                                     writes=[rbk, rpt])
                                sbank[(kbi, hh)] = (pt, rpt)

                        def pv(kbi):
                            (_, ulo, uhi, c0) = blocks[kbi]
                            for hh in range(2):
                                pt, rpt = sbank.pop((kbi, hh))
                                for (ua, ub) in plist[kbi]:
                                    m = ua // 512
                                    first = (m not in started)
                                    started.add(m)
                                    last = (last_kb[m] == kbi and hh == 1 and ub == max(x[1] for x in plist[kbi] if x[0] // 512 == m))
                                    nbk, dbk = banks[m % 2], banks[2 + m % 2]

                                    def mm(e, pt=pt, hh=hh, kbi=kbi, ua=ua, ub=ub, ulo=ulo, first=first, last=last, nbk=nbk, dbk=dbk):
                                        e.matmul(nbk[:, ua % 512:ua % 512 + ub - ua], lhsT=vp[:, kbi, hh, :], rhs=pt[:, ua - ulo:ub - ulo],
                                                 start=first, stop=last, skip_group_check=True)
                                        return e.matmul(dbk[:, ua % 512:ua % 512 + ub - ua], lhsT=e_h[:, hh, :], rhs=pt[:, ua - ulo:ub - ulo],
                                                        start=first, stop=last, skip_group_check=True)
                                    S.op("pe", mm, reads=[rvp, rpt, reh], writes=[r_bank[m % 2], r_bank[2 + m % 2]])
                            for m in sorted(set(x[0] // 512 for x in plist[kbi])):
                                if last_kb[m] != kbi:
                                    continue
                                if L >= 512:
                                    r = (512 * m) // L
                                    l0 = (512 * m) % L

                                    def view(t, r=r, l0=l0, d=d):
                                        return t[:, :].rearrange("p (l r) -> p r l", r=d)[:, r, l0:l0 + 512]

                                    def bview(b):
                                        return b[:, :]
                                else:
                                    nr = 512 // L
                                    r0 = m * nr

                                    def view(t, r0=r0, nr=nr, d=d):
                                        return t[:, :].rearrange("p (l r) -> p r l", r=d)[:, r0:r0 + nr, :]

                                    def bview(b, L=L):
                                        return b[:, :].rearrange("p (r l) -> p r l", l=L)
                                nbk, dbk = banks[m % 2], banks[2 + m % 2]
                                if g == 0:
                                    S.op("act", lambda e, nbk=nbk, view=view, bview=bview: e.activation(
                                        out=view(acc_n), in_=bview(nbk), func=AF.Copy), writes=[r_bank[m % 2], racc])
                                    S.op("dve", lambda e, dbk=dbk, view=view, bview=bview: e.tensor_copy(
                                        out=view(acc_d), in_=bview(dbk)), writes=[r_bank[2 + m % 2], racc])
                                else:
                                    def ad(e, nbk=nbk, dbk=dbk, view=view, bview=bview):
                                        e.tensor_tensor(out=view(acc_n), in0=bview(nbk), in1=view(acc_n), op=ALU.add)
                                        return e.tensor_tensor(out=view(acc_d), in0=bview(dbk), in1=view(acc_d), op=ALU.add)
                                    S.op("dve", ad, writes=[r_bank[m % 2], r_bank[2 + m % 2], racc])
                        qk(0)
                        qk(1)
                        for kbi in range(32):
                            pv(kbi)
                            if kbi + 2 < 32:
                                qk(kbi + 2)

                    S.op("act", lambda e: e.activation(out=acc_d[:, :], in_=acc_d[:, :], func=AF.Ln), writes=[racc])
                    S.op("act", lambda e: e.activation(out=acc_d[:, :], in_=acc_d[:, :], func=AF.Exp, scale=-1.0), writes=[racc])
                    S.op("dve", lambda e: e.tensor_tensor(out=ob[:, :], in0=acc_n[:, :], in1=acc_d[:, :], op=ALU.mult),
                         reads=[racc], writes=[rob])
                    S.op("pool", lambda e, p=p: e.dma_start(out=ot_d[p * 128:(p + 1) * 128, :], in_=ob[:, :]),
                         reads=[rob], writes=[r_out], dma=True)
                S.barrier(dummy)

        if stop_after == "A":
            phase_A(out)
        elif stop_after == "B":
            phase_A(x1_d)
            phase_ffn(0, x1_d, out)
        else:
            cast_gate[0] = phase_A(x1_d)
            late_casts()
            phase_ffn(0, x1_d, x2_d, hn2d=hn_d)
            phase_C()
            phase_ffn(1, x2_d, out, attn=True, final=True, mid2d=x3_d)
        S.op("sp", lambda e: e.nop(), reads=[r_out])
        S.emit()
    return nc


def _t5_bucket(rel):
    half = 16
    max_exact = 8
    n = np.abs(rel)
    side = np.where(rel > 0, half, 0)
    nf = np.maximum(n, 1).astype(np.float32)
    large = max_exact + (np.log(nf / max_exact) / np.log(1024 / max_exact) * (half - max_exact)).astype(np.int32)
    large = np.minimum(large, half - 1)
    return side + np.where(n < max_exact, n, large)


def _bias_tiles(rel_bias):
    r = np.arange(128)[:, None]
    c = np.arange(256)[None, :]
    j = r - c + 64
    valid = (j >= -64) & (j <= 64)
    out = np.empty((128, 24, 256), np.float32)
    for g in range(3):
        bid = _t5_bucket(DIL[g] * np.clip(j, -64, 64))
        for h in range(8):
            out[:, g * 8 + h, :] = np.where(valid, rel_bias[bid, g * 8 + h], np.float32(NEG))
    return out


def _pack_vecs(norm_w, final_norm, conv_w, ffn_conv_w, ffn_conv_b):
    cols = []
    for i in range(2):
        for j in range(2):
            cols.append(norm_w[i, j].reshape(KC, 128).T)
    cols.append(final_norm.reshape(KC, 128).T)
    for tap in range(3):
        cols.append(conv_w[0, tap].reshape(KC, 128).T)
    for i in range(2):
        for tap in range(3):
            cols.append(ffn_conv_w[i, tap].reshape(FCH, 128).T)
    for i in range(2):
        cols.append(ffn_conv_b[i].reshape(FCH, 128).T)
    v = np.ascontiguousarray(np.concatenate(cols, axis=1), dtype=np.float32)
    assert v.shape == (128, NV)
    return v


_PROG = {}


def kernel(x, norm_w, conv_in, conv_w, conv_out, attn_qkv, attn_out, rel_bias,
           ffn_up, ffn_conv_w, ffn_conv_b, ffn_down, final_norm, _stop_after="all", _cores=None):
    f = lambda a: np.asarray(a, dtype=np.float32)
    x = f(x)
    cores = list(range(8)) if _cores is None else list(_cores)
    shared = {
        "vecs": _pack_vecs(f(norm_w), f(final_norm), f(conv_w), f(ffn_conv_w), f(ffn_conv_b)),
        "ident": np.eye(128, dtype=np.float32),
        "biasT": _bias_tiles(f(rel_bias)),
        "conv_in": np.ascontiguousarray(f(conv_in)[0]),
        "conv_out": np.ascontiguousarray(f(conv_out)[0]),
        "attn_qkv": np.ascontiguousarray(f(attn_qkv)[0]),
        "attn_out": np.ascontiguousarray(f(attn_out)[0]),
        "ffn_up": np.ascontiguousarray(f(ffn_up)),
        "ffn_down": np.ascontiguousarray(f(ffn_down)),
    }
    in_maps = []
    for b in cores:
        m = dict(shared)
        m["xT"] = np.ascontiguousarray(x[b].T)
        in_maps.append(m)
    if _stop_after not in _PROG:
        _PROG[_stop_after] = build_program(_stop_after)
    res = run_bass_kernel_spmd(_PROG[_stop_after], in_maps, core_ids=list(range(len(cores))))
    outs = [np.ascontiguousarray(r["out"].T) for r in res.results]
    return np.stack(outs, axis=0).astype(np.float32)
```
